# Optimizing a Trainium2 kernel written in Bass

```python
import numpy as np
import jax, jax.numpy as jnp
from jax import lax

D_MODEL = 1024
BATCH = 16
SEQ = 4096
DEPTH = 2

N_BRANCH = 4
BRANCH_WIDTH = D_MODEL // N_BRANCH
HEAD_DIM = 64
NSA_HEADS = BRANCH_WIDTH // HEAD_DIM
NSA_CMP_LEN = 32
NSA_CMP_STRIDE = 16
NSA_CMP_HIDDEN = 2 * HEAD_DIM
NSA_SEL_LEN = 64
NSA_N_SEL = 16
NSA_WINDOW = 512
NSA_FORCE_BONUS = 1.0e4
POOL_WINDOWS = (2, 4, 8, 16)
POOL_GROUP = BRANCH_WIDTH // len(POOL_WINDOWS)
CONV_WIDTH = 3
FOX_HEADS = BRANCH_WIDTH // HEAD_DIM
ROPE_THETA = 500000.0
ROPE_DIM = HEAD_DIM // 4
Q_BLOCK = 128
NORM_EPS = 1e-6
MASK_VALUE = -1e30

IN_SPLITS = (
    NSA_HEADS * HEAD_DIM,
    6 * HEAD_DIM,
    3 * NSA_HEADS,
    BRANCH_WIDTH,
    3 * BRANCH_WIDTH,
    3 * BRANCH_WIDTH,
    FOX_HEADS,
    N_BRANCH * BRANCH_WIDTH,
    N_BRANCH * D_MODEL,
)
D_IN = sum(IN_SPLITS)

kernel_name = "hybrid_nsa_pool_conv_fox_parallel"


def rmsnorm(x, g):
    xf = x.astype(jnp.float32)
    y = xf * lax.rsqrt(jnp.mean(xf * xf, axis=-1, keepdims=True) + NORM_EPS)
    return (y * g.astype(jnp.float32)).astype(x.dtype)


def rope(x, pos):
    half = ROPE_DIM // 2
    inv = ROPE_THETA ** (-jnp.arange(half, dtype=jnp.float32) / half)
    ang = pos.astype(jnp.float32)[:, None] * inv[None, :]
    cos = jnp.cos(ang)[None, :, None, :].astype(x.dtype)
    sin = jnp.sin(ang)[None, :, None, :].astype(x.dtype)
    x1, x2, rest = x[..., :half], x[..., half:ROPE_DIM], x[..., ROPE_DIM:]
    return jnp.concatenate([x1 * cos - x2 * sin, x2 * cos + x1 * sin, rest], axis=-1)


def masked_softmax(s, mask):
    s = jnp.where(mask, s.astype(jnp.float32), MASK_VALUE)
    p = jax.nn.softmax(s, axis=-1)
    return jnp.where(mask, p, 0.0)


def to_blocks(a):
    B, T = a.shape[:2]
    a = a.reshape((B, T // Q_BLOCK, Q_BLOCK) + a.shape[2:])
    return jnp.moveaxis(a, 1, 0)


def from_blocks(a):
    a = jnp.moveaxis(a, 0, 1)
    return a.reshape((a.shape[0], a.shape[1] * a.shape[2]) + a.shape[3:])


def nsa_compress(k, pos_emb, w1, b1, w2, b2):
    B, T, hd = k.shape
    nc = (T - NSA_CMP_LEN) // NSA_CMP_STRIDE + 1
    idx = np.arange(nc)[:, None] * NSA_CMP_STRIDE + np.arange(NSA_CMP_LEN)[None, :]
    blocks = k[:, idx] + pos_emb
    flat = blocks.reshape(B, nc, NSA_CMP_LEN * hd)
    return jax.nn.silu(flat @ w1 + b1) @ w2 + b2


def nsa_mixer(q, kv, gate_logits, cmp_pos, cmp_w1, cmp_b1, cmp_w2, cmp_b2):
    B, T = q.shape[:2]
    pos = jnp.arange(T)
    q = rope(q.reshape(B, T, NSA_HEADS, HEAD_DIM), pos) * (HEAD_DIM ** -0.5)
    k_c, v_c, k_s, v_s, k_w, v_w = jnp.split(kv, 6, axis=-1)
    k_s = rope(k_s[:, :, None], pos)[:, :, 0]
    k_w = rope(k_w[:, :, None], pos)[:, :, 0]
    kc = nsa_compress(k_c, cmp_pos[0], cmp_w1[0], cmp_b1[0], cmp_w2[0], cmp_b2[0])
    vc = nsa_compress(v_c, cmp_pos[1], cmp_w1[1], cmp_b1[1], cmp_w2[1], cmp_b2[1])
    nc = kc.shape[1]
    cmp_end_np = np.arange(nc) * NSA_CMP_STRIDE + NSA_CMP_LEN - 1
    cmp_end = jnp.asarray(cmp_end_np)
    kc = rope(kc[:, :, None], cmp_end)[:, :, 0]
    n_blk = T // NSA_SEL_LEN
    n_sel = min(NSA_N_SEL, n_blk)
    ci = np.arange(nc)[:, None] * NSA_CMP_STRIDE
    sj = np.arange(n_blk)[None, :] * NSA_SEL_LEN
    overlap = jnp.asarray((ci < sj + NSA_SEL_LEN) & (ci + NSA_CMP_LEN > sj), jnp.float32)
    k_blk = k_s.reshape(B, n_blk, NSA_SEL_LEN, HEAD_DIM)
    v_blk = v_s.reshape(B, n_blk, NSA_SEL_LEN, HEAD_DIM)
    pad = ((0, 0), (NSA_WINDOW, 0), (0, 0))
    k_wp, v_wp = jnp.pad(k_w, pad), jnp.pad(v_w, pad)
    blk_ids = jnp.arange(n_blk)

    def one_block(args):
        qi, q_blk = args
        t = qi * Q_BLOCK + jnp.arange(Q_BLOCK)
        s = jnp.einsum('bqhd,bnd->bhqn', q_blk, kc)
        p_c = masked_softmax(s, (cmp_end[None, :] <= t[:, None])[None, None])
        o_cmp = jnp.einsum('bhqn,bnd->bqhd', p_c.astype(vc.dtype), vc)
        imp = jnp.einsum('bhqn,nj->bqj', p_c, overlap)
        jt = (t // NSA_SEL_LEN)[:, None]
        j = blk_ids[None, :]
        forced = (j == 0) | (j == jt) | (j == jt - 1)
        valid = j * NSA_SEL_LEN <= t[:, None]
        imp = jnp.where(valid, jnp.where(forced, imp + NSA_FORCE_BONUS, imp), MASK_VALUE)
        _, sel = lax.top_k(imp, n_sel)
        ks = jax.vmap(lambda kb, ib: kb[ib])(k_blk, sel)
        vs = jax.vmap(lambda vb, ib: vb[ib])(v_blk, sel)
        kpos = sel[..., None] * NSA_SEL_LEN + jnp.arange(NSA_SEL_LEN)
        m_s = (kpos <= t[None, :, None, None]).reshape(B, 1, Q_BLOCK, n_sel * NSA_SEL_LEN)
        s = jnp.einsum('bqhd,bqnkd->bhqnk', q_blk, ks).reshape(B, NSA_HEADS, Q_BLOCK, n_sel * NSA_SEL_LEN)
        p_s = masked_softmax(s, m_s).reshape(B, NSA_HEADS, Q_BLOCK, n_sel, NSA_SEL_LEN)
        o_slc = jnp.einsum('bhqnk,bqnkd->bqhd', p_s.astype(vs.dtype), vs)
        kw = lax.dynamic_slice_in_dim(k_wp, qi * Q_BLOCK, Q_BLOCK + NSA_WINDOW, axis=1)
        vw = lax.dynamic_slice_in_dim(v_wp, qi * Q_BLOCK, Q_BLOCK + NSA_WINDOW, axis=1)
        wpos = qi * Q_BLOCK - NSA_WINDOW + jnp.arange(Q_BLOCK + NSA_WINDOW)
        m_w = (wpos[None, :] <= t[:, None]) & (wpos[None, :] > t[:, None] - NSA_WINDOW) & (wpos[None, :] >= 0)
        s = jnp.einsum('bqhd,bkd->bhqk', q_blk, kw)
        p_w = masked_softmax(s, m_w[None, None])
        o_win = jnp.einsum('bhqk,bkd->bqhd', p_w.astype(vw.dtype), vw)
        return o_cmp, o_slc, o_win

    o_cmp, o_slc, o_win = lax.map(one_block, (jnp.arange(T // Q_BLOCK), to_blocks(q)))
    g = jax.nn.sigmoid(gate_logits.reshape(B, T, 3, NSA_HEADS))[..., None]
    o = (g[:, :, 0] * from_blocks(o_cmp) + g[:, :, 1] * from_blocks(o_slc)
         + g[:, :, 2] * from_blocks(o_win))
    return o.reshape(B, T, NSA_HEADS * HEAD_DIM)


def pool_mixer(u, pool_w, pool_scale):
    B, T, C = u.shape
    cs = jnp.concatenate([jnp.zeros((B, 1, C), jnp.float32),
                          jnp.cumsum(u.astype(jnp.float32), axis=1)], axis=1)
    outs = []
    for gi, w in enumerate(POOL_WINDOWS):
        sl = slice(gi * POOL_GROUP, (gi + 1) * POOL_GROUP)
        csg = cs[..., sl]
        lag = jnp.concatenate([jnp.zeros((B, w - 1, POOL_GROUP), jnp.float32),
                               csg[:, :T + 1 - w]], axis=1)
        cnt = jnp.minimum(jnp.arange(1, T + 1), w).astype(jnp.float32)[None, :, None]
        outs.append(((csg[:, 1:] - lag) / cnt).astype(u.dtype) - u[..., sl])
    pooled = jnp.stack(outs, axis=2)
    mixed = jnp.einsum('btgc,gcd->btgd', pooled, pool_w).reshape(B, T, C)
    return mixed * pool_scale


def conv_mixer(xin, b_gate, c_gate, conv_w):
    T = xin.shape[1]
    u = jnp.pad(c_gate * xin, ((0, 0), (CONV_WIDTH - 1, 0), (0, 0)))
    y = sum(u[:, k:k + T] * conv_w[k] for k in range(CONV_WIDTH))
    return b_gate * y


def fox_mixer(q, k, v, f_logit, f_bias):
    B, T = q.shape[:2]
    q = q.reshape(B, T, FOX_HEADS, HEAD_DIM) * (HEAD_DIM ** -0.5)
    k = k.reshape(B, T, FOX_HEADS, HEAD_DIM)
    v = v.reshape(B, T, FOX_HEADS, HEAD_DIM)
    logf = jax.nn.log_sigmoid(f_logit.astype(jnp.float32) + f_bias.astype(jnp.float32))
    c = jnp.cumsum(logf, axis=1)
    c_keys = jnp.transpose(c, (0, 2, 1))
    kpos = jnp.arange(T)

    def one_block(args):
        qi, q_blk, c_blk = args
        t = qi * Q_BLOCK + jnp.arange(Q_BLOCK)
        s = jnp.einsum('bqhd,bkhd->bhqk', q_blk, k).astype(jnp.float32)
        s = s + jnp.transpose(c_blk, (0, 2, 1))[..., None] - c_keys[:, :, None, :]
        p = masked_softmax(s, (kpos[None, :] <= t[:, None])[None, None])
        return jnp.einsum('bhqk,bkhd->bqhd', p.astype(v.dtype), v)

    o = lax.map(one_block, (jnp.arange(T // Q_BLOCK), to_blocks(q), to_blocks(c)))
    return from_blocks(o).reshape(B, T, FOX_HEADS * HEAD_DIM)


def hybrid_layer(x, norm_g, w_in, fox_f_bias, cmp_pos, cmp_w1, cmp_b1, cmp_w2, cmp_b2,
                 pool_w, pool_scale, conv_w, w_branch, w_out):
    B, T, _ = x.shape
    h = rmsnorm(x, norm_g)
    proj = h @ w_in
    split_at = np.cumsum(IN_SPLITS)[:-1].tolist()
    (nsa_q, nsa_kv, nsa_g, pool_in, conv_in, fox_qkv, fox_f,
     gate_in, merge_in) = jnp.split(proj, split_at, axis=-1)
    o_nsa = nsa_mixer(nsa_q, nsa_kv, nsa_g, cmp_pos, cmp_w1, cmp_b1, cmp_w2, cmp_b2)
    o_pool = pool_mixer(pool_in, pool_w, pool_scale)
    c_x, c_b, c_c = jnp.split(conv_in, 3, axis=-1)
    o_conv = conv_mixer(c_x, c_b, c_c, conv_w)
    f_q, f_k, f_v = jnp.split(fox_qkv, 3, axis=-1)
    o_fox = fox_mixer(f_q, f_k, f_v, fox_f, fox_f_bias)
    gates = gate_in.reshape(B, T, N_BRANCH, BRANCH_WIDTH)
    merge = merge_in.reshape(B, T, N_BRANCH, D_MODEL)
    acc = None
    for i, o in enumerate((o_nsa, o_pool, o_conv, o_fox)):
        branch = (o * jax.nn.silu(gates[:, :, i])) @ w_branch[i]
        term = jax.nn.sigmoid(merge[:, :, i]) * branch
        acc = term if acc is None else acc + term
    return x + acc @ w_out


def setup_inputs(seed: int = 0) -> dict:
    key = jax.random.key(seed)
    ks = jax.random.split(key, 20)
    f32 = jnp.float32
    nrm = lambda k, shape, scale: jax.random.normal(k, shape, f32) * scale
    cmp_in = NSA_CMP_LEN * HEAD_DIM
    return {
        "x": nrm(ks[0], (BATCH, SEQ, D_MODEL), 1.0),
        "norm_g": 1.0 + nrm(ks[1], (DEPTH, D_MODEL), 0.05),
        "w_in": nrm(ks[2], (DEPTH, D_MODEL, D_IN), D_MODEL ** -0.5),
        "fox_f_bias": 3.0 + nrm(ks[3], (DEPTH, FOX_HEADS), 0.5),
        "cmp_pos": nrm(ks[4], (DEPTH, 2, NSA_CMP_LEN, HEAD_DIM), 0.1),
        "cmp_w1": nrm(ks[5], (DEPTH, 2, cmp_in, NSA_CMP_HIDDEN), cmp_in ** -0.5),
        "cmp_b1": nrm(ks[6], (DEPTH, 2, NSA_CMP_HIDDEN), 0.02),
        "cmp_w2": nrm(ks[7], (DEPTH, 2, NSA_CMP_HIDDEN, HEAD_DIM), NSA_CMP_HIDDEN ** -0.5),
        "cmp_b2": nrm(ks[8], (DEPTH, 2, HEAD_DIM), 0.02),
        "pool_w": nrm(ks[9], (DEPTH, len(POOL_WINDOWS), POOL_GROUP, POOL_GROUP), POOL_GROUP ** -0.5),
        "pool_scale": 1.0 + nrm(ks[10], (DEPTH, BRANCH_WIDTH), 0.1),
        "conv_w": nrm(ks[11], (DEPTH, CONV_WIDTH, BRANCH_WIDTH), CONV_WIDTH ** -0.5),
        "w_branch": nrm(ks[12], (DEPTH, N_BRANCH, BRANCH_WIDTH, D_MODEL), BRANCH_WIDTH ** -0.5),
        "w_out": nrm(ks[13], (DEPTH, D_MODEL, D_MODEL), D_MODEL ** -0.5),
        "final_norm_g": 1.0 + nrm(ks[14], (D_MODEL,), 0.05),
    }


def reference(x, norm_g, w_in, fox_f_bias, cmp_pos, cmp_w1, cmp_b1, cmp_w2, cmp_b2,
              pool_w, pool_scale, conv_w, w_branch, w_out, final_norm_g):
    for l in range(DEPTH):
        x = hybrid_layer(x, norm_g[l], w_in[l], fox_f_bias[l], cmp_pos[l], cmp_w1[l],
                         cmp_b1[l], cmp_w2[l], cmp_b2[l], pool_w[l], pool_scale[l],
                         conv_w[l], w_branch[l], w_out[l])
    return rmsnorm(x, final_norm_g)
```

```python
import contextlib
import numpy as np
import concourse.bass as bass
import concourse.mybir as mybir
from concourse.bass_utils import run_bass_kernel_spmd

F32 = mybir.dt.float32
BF16 = mybir.dt.bfloat16
AF = mybir.ActivationFunctionType
ALU = mybir.AluOpType

D = 1024
HD = 64
N_CORES = 8
STAGE = 99
RUN_KW = {}
LAST_RES = None
DEBUG = False

ENGS = ("tensor", "vector", "scalar", "gpsimd", "sync")
N_DMA_SEMS = 40


class Op:
    __slots__ = ("eng", "fn", "deps", "idx", "dma", "marked", "inc", "dsem", "dval")

    def __init__(self, eng, fn, dma):
        self.eng = eng
        self.fn = fn
        self.deps = []
        self.dma = dma
        self.marked = False
        self.inc = 0
        self.dsem = None
        self.dval = 0


class _Rec:
    def __getattr__(self, name):
        def f(*a, **kw):
            return (name, a, kw)
        return f


_REC = _Rec()


class Sched:
    def __init__(self, nc):
        self.nc = nc
        self.ops = {e: [] for e in ENGS}
        self.last_writer = {}
        self.readers = {}
        self.n = 0
        self.enabled = True

    def add(self, eng, fn, reads=(), writes=(), dma=False):
        if not self.enabled:
            return None
        call = fn(_REC)
        op = Op(eng, call, dma)
        op.idx = self.n
        self.n += 1
        deps = {}
        for k in reads:
            w = self.last_writer.get(k)
            if w is not None:
                deps[w.idx] = w
        for k in writes:
            w = self.last_writer.get(k)
            if w is not None:
                deps[w.idx] = w
            lastr = {}
            for r in self.readers.get(k, ()):
                if r.dma:
                    deps[r.idx] = r
                    continue
                if r.eng == eng and not dma:
                    continue
                lastr[r.eng] = r
            for r in lastr.values():
                deps[r.idx] = r
        for d in deps.values():
            if d.eng == eng and not d.dma and not dma and eng == "tensor":
                continue
            op.deps.append(d)
            d.marked = True
        for k in reads:
            self.readers.setdefault(k, []).append(op)
        for k in writes:
            self.last_writer[k] = op
            self.readers[k] = []
        self.ops[eng].append(op)
        return op

    def dma(self, out, in_, reads=(), writes=(), eng="sync"):
        return self.add(eng, lambda e: e.dma_start(out=out, in_=in_), reads, writes, dma=True)

    def finalize(self, final_wait_eng="sync"):
        nc = self.nc
        for e in ENGS:
            c = 0
            for op in self.ops[e]:
                if not op.dma and op.marked:
                    c += 1
                    op.inc = c
        with contextlib.ExitStack() as st:
            esem = {e: st.enter_context(nc.semaphore("s_" + e)) for e in ENGS}
            dsems = [st.enter_context(nc.semaphore("d%d" % i)) for i in range(N_DMA_SEMS)]
            dtot = [0] * N_DMA_SEMS
            all_dma = sorted([op for e in ENGS for op in self.ops[e] if op.dma], key=lambda o: o.idx)
            for i, op in enumerate(all_dma):
                s = i % N_DMA_SEMS
                dtot[s] += 16
                op.dsem = s
                op.dval = dtot[s]
            block = st.enter_context(nc.Block())
            ops = self.ops

            def emit_engine(ename, eng):
                waited = {}

                def wait(key, sem, val):
                    if waited.get(key, 0) >= val:
                        return
                    waited[key] = val
                    eng.wait_ge(sem, val)

                for op in ops[ename]:
                    for d in op.deps:
                        if d.dma:
                            wait(("d", d.dsem), dsems[d.dsem], d.dval)
                        else:
                            wait(("e", d.eng), esem[d.eng], d.inc)
                    if op.dma:
                        if op.dval > 16:
                            wait(("d", op.dsem), dsems[op.dsem], op.dval - 16)
                        ins = getattr(eng, op.fn[0])(*op.fn[1], **op.fn[2])
                        ins.then_inc(dsems[op.dsem], 16)
                    else:
                        ins = getattr(eng, op.fn[0])(*op.fn[1], **op.fn[2])
                        if op.marked:
                            ins.then_inc(esem[ename], 1)
                if ename == final_wait_eng:
                    for s in range(N_DMA_SEMS):
                        if dtot[s] > 0:
                            wait(("d", s), dsems[s], dtot[s])

            @block.sync
            def _(eng):
                emit_engine("sync", eng)

            @block.tensor
            def _(eng):
                emit_engine("tensor", eng)

            @block.vector
            def _(eng):
                emit_engine("vector", eng)

            @block.scalar
            def _(eng):
                emit_engine("scalar", eng)

            @block.gpsimd
            def _(eng):
                emit_engine("gpsimd", eng)


O_Q, O_KV, O_G, O_POOL, O_CONV, O_F, O_FF, O_GATE, O_MERGE = 0, 256, 640, 652, 908, 1676, 2444, 2448, 3472
NU_MAIN = 17


def _sw(cols):
    cols = np.asarray(cols)
    out = cols.copy()
    out[0:8] = cols[8:16]
    out[8:16] = cols[0:8]
    return out


def _unit_cols():
    ar = np.arange
    q = [O_Q + 64 * h + ar(64) for h in range(4)]
    k_c, v_c, k_s, v_s, k_w, v_w = [O_KV + 64 * i + ar(64) for i in range(6)]
    f_q = O_F + ar(256)
    f_k = O_F + 256 + ar(256)
    f_v = O_F + 512 + ar(256)
    cat = np.concatenate
    units = []
    units.append(cat([q[0], q[1], _sw(q[0]), _sw(q[1]), q[2], q[3], _sw(q[2]), _sw(q[3])]))
    units.append(cat([k_s, k_s, _sw(k_s), _sw(k_s), k_w, k_w, _sw(k_w), _sw(k_w)]))
    units.append(cat([k_c, v_c, f_q[0:128], f_q[128:256], f_k[0:128]]))
    units.append(cat([f_k[128:256], v_s, v_w, f_v]))
    units.append(cat([O_POOL + ar(256), O_CONV + ar(256)]))
    units.append(cat([O_CONV + 256 + ar(256), O_CONV + 512 + ar(256)]))
    units.append(cat([O_GATE + 256 + ar(256), O_GATE + 512 + ar(256)]))
    units.append(cat([O_GATE + ar(256), O_GATE + 768 + ar(256)]))
    units.append(cat([O_G + ar(12), O_FF + ar(4)]))
    for c in range(8):
        units.append(O_MERGE + c * 512 + ar(512))
    return units


def prep_layer(inp, l):
    f = np.float32
    w_in = np.asarray(inp["w_in"][l], f)
    wmain = np.zeros((NU_MAIN, 128, 8, 512), f)
    for u, cols in enumerate(_unit_cols()):
        w = w_in[:, cols]
        wmain[u, :, :, :w.shape[1]] = w.reshape(8, 128, -1).transpose(1, 0, 2)
    wb = np.asarray(inp["w_branch"][l], f)
    wbr = np.zeros((8, 128, 2, 512), f)
    for c in range(8):
        i, half = c // 2, c % 2
        wbr[c] = wb[i][:, half * 512:(half + 1) * 512].reshape(2, 128, 512).transpose(1, 0, 2)
    wo_ = np.asarray(inp["w_out"][l], f)
    wo = np.zeros((2, 128, 8, 512), f)
    for half in range(2):
        wo[half] = wo_[:, half * 512:(half + 1) * 512].reshape(8, 128, 512).transpose(1, 0, 2)
    w1 = np.asarray(inp["cmp_w1"][l], f)
    w1kv = np.zeros((128, 32, 128), f)
    poskv = np.zeros((128, 32), f)
    pos = np.asarray(inp["cmp_pos"][l], f)
    for kv in range(2):
        w1kv[64 * kv:64 * kv + 64] = w1[kv].reshape(32, 64, 128).transpose(1, 0, 2)
        poskv[64 * kv:64 * kv + 64] = pos[kv].T
    w2 = np.asarray(inp["cmp_w2"][l], f)
    b2 = np.asarray(inp["cmp_b2"][l], f)
    swi = _sw(np.arange(64))
    w2k = np.zeros((128, 2, 128), f)
    w2k[:, 0, :] = np.concatenate([w2[0], w2[0]], axis=1)
    w2k[:, 1, :] = np.concatenate([w2[0][:, swi], w2[0][:, swi]], axis=1)
    pw = np.asarray(inp["pool_w"][l], f)
    pool_bd = np.zeros((128, 2, 128), f)
    for c in range(2):
        for r in range(2):
            pool_bd[64 * r:64 * r + 64, c, 64 * r:64 * r + 64] = pw[2 * c + r]
    small = np.zeros((128, 24), f)
    small[:, 0:8] = np.asarray(inp["norm_g"][l], f).reshape(8, 128).T
    small[:, 8:10] = np.asarray(inp["cmp_b1"][l], f).T
    small[:, 10] = np.concatenate([b2[0], b2[0]])
    small[:, 11] = np.concatenate([b2[0][swi], b2[0][swi]])
    small[:, 12:14] = np.asarray(inp["pool_scale"][l], f).reshape(2, 128).T
    cw = np.asarray(inp["conv_w"][l], f)
    for c in range(2):
        small[:, 14 + 3 * c:17 + 3 * c] = cw[:, c * 128:(c + 1) * 128].T
    return dict(wmain=wmain, wbr=wbr, wo=wo, w1kv=w1kv, poskv=poskv, w2k=w2k,
                w2v=np.ascontiguousarray(w2[1]), b2v=np.ascontiguousarray(b2[1][None, :]),
                pool_bd=pool_bd, small=small)


def prep_consts(T, depth, inp):
    f = np.float32
    NT = T // 128
    NCP = T // 16
    c = {}
    c["ident"] = np.eye(128, dtype=f)
    p = np.arange(128)[:, None]
    fr = np.arange(128)[None, :]
    c["tri"] = (fr >= p).astype(f)
    c["ntri"] = (fr < p).astype(f)
    c["ones"] = np.ones((128, 128), f)
    kg = np.arange(T)[None, :]
    c["efull"] = (kg // 64 == np.arange(64)[:, None]).astype(f)
    half = 8
    inv = (500000.0 ** (-np.arange(half, dtype=np.float32) / half)).astype(np.float32)

    def rope_tab(posv):
        ang = posv.astype(np.float32)[:, None] * inv[None, :]
        cs = np.cos(ang).astype(f).T
        sn = np.sin(ang).astype(f).T
        ct = np.ones((64, posv.shape[0]), f)
        st = np.zeros((64, posv.shape[0]), f)
        ct[0:8] = cs
        ct[8:16] = cs
        st[0:8] = -sn
        st[8:16] = sn
        return np.concatenate([ct, ct], 0), np.concatenate([st, st], 0)

    c["ropec"], c["ropes"] = rope_tab(np.arange(T))
    c["cmpc"], c["cmps"] = rope_tab(np.arange(NCP) * 16 + 31)
    c["cmask"] = (16 * np.arange(128)[:, None] + 31 <= np.arange(T)[None, :]).astype(f)
    t = np.arange(T)[:, None]
    j = np.arange(T // 64)[None, :]
    jt = t // 64
    forced = (j == 0) | (j == jt) | (j == jt - 1)
    valid = j * 64 <= t
    addt = np.where(valid, np.where(forced, 1.0e4, 0.0), -1.0e30).astype(f)
    addtab = np.zeros((NT, 128, 64), f)
    addtab[:, :, :T // 64] = addt.reshape(NT, 128, T // 64)
    if T // 64 < 64:
        addtab[:, :, T // 64:] = -1.0e30
    c["addtab"] = addtab
    n = np.arange(NCP)[:, None]
    ovl = ((n * 16 < j * 64 + 64) & (n * 16 + 32 > j * 64)).astype(f)
    ovl[NCP - 1] = 0.0
    ov = np.zeros((NCP, 64), f)
    ov[:, :T // 64] = ovl
    nct = max(1, NCP // 128)
    c["ovl"] = np.ascontiguousarray(ov.reshape(nct, -1, 64).transpose(1, 0, 2)) if NCP >= 128 else ov[:, None, :]
    pw = np.array([2, 4, 8, 16], f)
    invw = np.zeros((128, 2), f)
    invc = np.zeros((128, 2, 16), f)
    for ch in range(2):
        for r in range(2):
            w = pw[2 * ch + r]
            invw[64 * r:64 * r + 64, ch] = 1.0 / w
            invc[64 * r:64 * r + 64, ch, :] = 1.0 / np.minimum(np.arange(1, 17), w)
    c["invw"] = invw
    c["invc"] = invc
    c["fgbc"] = np.ascontiguousarray(np.broadcast_to(np.asarray(inp["final_norm_g"], f)[None, :], (128, D)))
    c["fbbc"] = np.ascontiguousarray(np.broadcast_to(np.asarray(inp["fox_f_bias"], f)[:depth].reshape(1, -1), (128, depth * 4)))
    return c


LAYER_KEYS = ("wmain", "wbr", "wo", "w1kv", "poskv", "w2k", "w2v", "b2v", "pool_bd", "small")


def build_program(T, DEPTH, NSEQ, QI, const_shapes, layer_shapes):
    G = 128 * QI
    NG = T // G
    NT = T // 128
    NCP = T // 16
    NC = NCP - 1
    NCT = max(1, NCP // 128)
    NCW = min(128, NCP)
    NB = T // 64

    nc = bass.Bass("TRN2", target_bir_lowering=False)
    xin = nc.dram_tensor("x", [NSEQ * T, D], F32, kind="ExternalInput").ap()
    yout = nc.dram_tensor("y", [NSEQ * T, D], F32, kind="ExternalOutput").ap()
    dbg = nc.dram_tensor("dbg", [NSEQ * T, 512], F32, kind="ExternalOutput").ap() if DEBUG else None
    xmid = [nc.dram_tensor("xmid%d" % i, [NSEQ * T, D], F32).ap() for i in range(max(0, DEPTH - 1))]
    cd = {k: nc.dram_tensor("c_" + k, list(s), F32, kind="ExternalInput").ap() for k, s in const_shapes.items()}
    ld = [{k: nc.dram_tensor("l%d_%s" % (l, k), list(s), F32, kind="ExternalInput").ap()
           for k, s in layer_shapes.items()} for l in range(DEPTH)]

    BSH = dict(wmain=[NU_MAIN, 128, 8, 512], wbr=[8, 128, 2, 512], wo=[2, 128, 8, 512], w1kv=[128, 32, 128],
               poskv=[128, 32], w2k=[128, 2, 128], w2v=[128, 64], b2v=[1, 64], pool_bd=[128, 2, 128])
    wdb = [{k: nc.dram_tensor("b%d_%s" % (l, k), shp, BF16).ap() for k, shp in BSH.items()} for l in range(DEPTH)]

    st = contextlib.ExitStack()
    with st:
        def sb(name, shape, dt=F32):
            return st.enter_context(nc.sbuf_tensor(name, list(shape), dt))

        def ps(name, shape, dt=F32):
            return st.enter_context(nc.psum_tensor(name, list(shape), dt))

        S = Sched(nc)

        def V(fn, r, w):
            S.add("vector", fn, r, w)

        def A(fn, r, w):
            S.add("scalar", fn, r, w)

        def MM(out, lhsT, rhs, start, stop, r, w, skip=False):
            S.add("tensor", lambda e: e.matmul(out, lhsT=lhsT, rhs=rhs, start=start, stop=stop,
                                               skip_group_check=skip), r, w)

        psP = ps("psP", [128, 2, 512])
        psS = ps("psS", [128, 2, 512])
        psM = ps("psM", [128, 512])
        acc = ps("acc", [128, 3, 512])

        def slot(i, w):
            per = 512 // w
            return acc[:, i // per, (i % per) * w:(i % per) * w + w]

        def slot_keys(n, w):
            per = 512 // w
            return [("acc", b) for b in range((n + per - 1) // per)]

        ident = sb("ident", [128, 128], BF16)
        tri = sb("tri", [128, 128], BF16)
        ntri = sb("ntri", [128, 128], BF16)
        trif = sb("trif", [128, 128], F32)
        onesf = sb("onesf", [128, 128], F32)
        onesb = sb("onesb", [1, 128], BF16)
        efull = sb("efull", [64, T], BF16)
        cmpc = sb("cmpc", [128, NCP], F32)
        cmps = sb("cmps", [128, NCP], F32)
        invw = sb("invw", [128, 2], F32)
        invc = sb("invc", [128, 2, 16], F32)
        fgbc = sb("fgbc", [128, D], F32)
        fbbc = sb("fbbc", [128, DEPTH * 4], F32)
        S.dma(ident[:], cd["ident"], writes=["ident"], eng="gpsimd")
        S.dma(tri[:], cd["tri"], writes=["tri"], eng="gpsimd")
        S.dma(ntri[:], cd["ntri"], writes=["ntri"], eng="gpsimd")
        S.dma(onesb[:], cd["ones"][0:1, :], writes=["onesb"], eng="gpsimd")
        S.dma(efull[:], cd["efull"], writes=["efull"], eng="gpsimd")
        S.dma(trif[:], cd["tri"], writes=["trif"])
        S.dma(onesf[:], cd["ones"], writes=["onesf"])
        S.dma(cmpc[:], cd["cmpc"], writes=["cmpc"])
        S.dma(cmps[:], cd["cmps"], writes=["cmps"])
        S.dma(invw[:], cd["invw"], writes=["invw"])
        S.dma(invc[:], cd["invc"], writes=["invc"])
        S.dma(fgbc[:], cd["fgbc"], writes=["fgbc"])
        S.dma(fbbc[:], cd["fbbc"], writes=["fbbc"])

        for l in range(DEPTH):
            for k in ("w1kv", "poskv", "w2k", "w2v", "b2v", "pool_bd"):
                S.dma(wdb[l][k], ld[l][k], writes=[("wd", l, k, 0)], eng="gpsimd")
            for k, n in (("wmain", NU_MAIN), ("wbr", 8), ("wo", 2)):
                for u in range(n):
                    S.dma(wdb[l][k][u], ld[l][k][u], writes=[("wd", l, k, u)], eng="gpsimd")
        small = sb("small", [128, 24], F32)
        w1kv = sb("w1kv", [128, 32, 128], BF16)
        poskv = sb("poskv", [128, 32], BF16)
        w2k = sb("w2k", [128, 2, 128], BF16)
        w2v = sb("w2v", [128, 64], BF16)
        b2v = sb("b2v", [1, 64], BF16)
        pool_bd = sb("pool_bd", [128, 2, 128], BF16)
        c1 = sb("c1", [128, 2], F32)

        KS2 = sb("KS2", [128, T], BF16)
        KW2 = sb("KW2", [128, T], BF16)
        KCVC = sb("KCVC", [128, T], BF16)
        FK = [sb("FK%d" % i, [128, T], BF16) for i in range(2)]
        VS = sb("VS", [128, NT, 65], BF16)
        VW = sb("VW", [128, NT, 65], BF16)
        FV = sb("FV", [128, NT, 4, 65], BF16)
        KCC = sb("KCC", [128, NCT * 128], BF16)
        VCA = sb("VCA", [128, NCT, 129], BF16)
        CKR = sb("CKR", [128, NT, 8], F32)
        RSUM = sb("RSUM", [128, 4], F32)

        NWB = 3
        wb = [sb("wb%d" % i, [128, 10, 512], BF16) for i in range(NWB)]
        wcnt = [0]
        wissued = [0]
        NG_ = T // (128 * QI)
        unit_seq = []
        for l_ in range(DEPTH):
            per = []
            for u in range(9):
                ncol = 16 if u == 8 else 512
                per.append([(0, 8, ncol, wdb[l_]["wmain"][u][:, :, 0:ncol], ("wd", l_, "wmain", u))])
            for c in range(8):
                per.append([(0, 8, 512, wdb[l_]["wmain"][9 + c], ("wd", l_, "wmain", 9 + c)),
                            (8, 10, 512, wdb[l_]["wbr"][c], ("wd", l_, "wbr", c))])
            for hf in range(2):
                per.append([(0, 8, 512, wdb[l_]["wo"][hf], ("wd", l_, "wo", hf))])
            for _ in range(NSEQ * NG_):
                unit_seq.extend(per)

        def next_unit():
            k = wcnt[0]
            wcnt[0] += 1
            while wissued[0] < min(len(unit_seq), k + NWB) and wissued[0] <= k + 2:
                j = wissued[0]
                wissued[0] += 1
                if not S.enabled:
                    continue
                for lo, hi, ncol, ap, dkey in unit_seq[j]:
                    S.dma(wb[j % NWB][:, lo:hi, 0:ncol], ap, reads=[dkey], writes=[("wb", j % NWB)])
            return wb[k % NWB], ("wb", k % NWB)

        xg = sb("xg", [128, QI, D], F32)
        sq = sb("sq", [128, D], F32)
        ss = sb("ss", [128, QI], F32)
        rstd = sb("rstd", [128, QI], F32)
        xn = sb("xn", [128, D], BF16)
        hT = sb("hT", [128, 8, G], BF16)
        QN = [sb("QN%d" % i, [128, G], BF16) for i in range(2)]
        FQ = [sb("FQ%d" % i, [128, G], BF16) for i in range(2)]
        ropeC = sb("ropeC", [128, G], F32)
        ropeS = sb("ropeS", [128, G], F32)
        rt1 = sb("rt1", [128, G], F32)
        rt2 = sb("rt2", [128, G], F32)
        GPC = sb("GPC", [128, 4, G], BF16)
        GNF = sb("GNF", [128, QI, 512], BF16)
        NGt = sb("NGt", [128, QI, 12], F32)
        LF = sb("LF", [128, QI, 4], F32)
        LFt = sb("LFt", [128, QI, 4], F32)
        PIN = sb("PIN", [128, 2, 16 + G], F32)
        CX = sb("CX", [128, 2, G], F32)
        CB = sb("CB", [128, 2, G], F32)
        CU = sb("CU", [128, 2, 2 + G], F32)
        PA = sb("PA", [128, 2, 16 + G], F32)
        PB = sb("PB", [128, 2, 16 + G], F32)
        PL = sb("PL", [128, 2, G], BF16)
        PT16 = sb("PT16", [128, 16], F32)
        CY = sb("CY", [128, G], F32)
        ogT = sb("ogT", [128, 8, G], BF16)
        ACCM = sb("ACCM", [128, QI, D], F32)
        ACCB = sb("ACCB", [128, D], BF16)
        ONSA = sb("ONSA", [128, QI, 256], F32)
        OFOX = sb("OFOX", [128, QI, 256], F32)
        OG = sb("OG", [128, 256], BF16)
        IMP = sb("IMP", [128, QI, 64], F32)
        IMPM = sb("IMPM", [128, 64], F32)
        IMPM2 = sb("IMPM2", [128, 64], F32)
        addt = sb("addt", [128, 64], F32)
        m8a = sb("m8a", [128, 8], F32)
        m8b = sb("m8b", [128, 8], F32)
        sel = sb("sel", [128, 64], BF16)
        selT = sb("selT", [64, G], BF16)
        ET = [sb("ET%d" % i, [128, G], BF16) for i in range(2)]
        PTt = [sb("PTt%d" % i, [128, G], BF16) for i in range(2)]
        PTC = [sb("PTC%d" % i, [128, G], BF16) for i in range(NCT)]
        MT = [sb("MT%d" % i, [128, G], BF16) for i in range(2)]
        cm = [sb("cm%d" % i, [128, G], F32) for i in range(NCT)]
        BT = sb("BT", [128, 4, QI, NT], F32)
        SM = sb("SM", [128, 512], F32)
        TMPt = sb("TMPt", [128, 512], F32)
        den = sb("den", [128, 1], F32)
        rden = sb("rden", [128, 1], F32)
        scl = sb("scl", [128, 1], F32)
        hid = [sb("hid%d" % i, [128, 32], BF16) for i in range(2)]
        kt1 = sb("kt1", [128, 32], F32)
        kt2 = sb("kt2", [128, 32], F32)
        vstage = sb("vstage", [32, 64], BF16)
        YO = sb("YO", [128, D], F32)
        fence = sb("fence", [128, 1], F32)
        den8 = sb("den8", [128, 16], F32)
        rden8 = sb("rden8", [128, 16], F32)
        scl8 = sb("scl8", [128, 16], F32)
        U8s = sb("U8s", [128, 64], F32)
        epsb = sb("epsb", [128, 1], F32)
        oneb = sb("oneb", [128, 1], F32)
        V(lambda e: e.memset(epsb[:], 1e-6), [], ["epsb"])
        V(lambda e: e.memset(oneb[:], 1.0), [], ["oneb"])
        cnt = {"e": 0, "p": 0, "m": 0, "s": 0, "t": 0}

        V(lambda e: e.memset(VS[:, :, 64:65], 1.0), [], ["VS1"])
        V(lambda e: e.memset(VW[:, :, 64:65], 1.0), [], ["VW1"])
        V(lambda e: e.memset(FV[:, :, :, 64:65], 1.0), [], ["FV1"])
        V(lambda e: e.memset(VCA[:, :, 64:65], 1.0), [], ["VCA1"])
        V(lambda e: e.memset(VCA[:, :, 65:129], 0.0), [], ["VCAo"])
        S.dma(VCA[0:NCW, :, 65:129], cd["ovl"], writes=["VCAo"], eng="gpsimd")

        for l in range(DEPTH):
            L = ld[l]
            lastl = (l == DEPTH - 1)
            xsrc = xin if l == 0 else xmid[l - 1]
            xdst = yout if lastl else xmid[l]
            S.dma(small[:], L["small"], writes=["small"])
            S.dma(w1kv[:], wdb[l]["w1kv"], reads=[("wd", l, "w1kv", 0)], writes=["w1kv"])
            S.dma(poskv[:], wdb[l]["poskv"], reads=[("wd", l, "poskv", 0)], writes=["poskv"])
            S.dma(w2k[:], wdb[l]["w2k"], reads=[("wd", l, "w2k", 0)], writes=["w2k"])
            S.dma(w2v[:], wdb[l]["w2v"], reads=[("wd", l, "w2v", 0)], writes=["w2v"])
            S.dma(b2v[:], wdb[l]["b2v"], reads=[("wd", l, "b2v", 0)], writes=["b2v"])
            S.dma(pool_bd[:], wdb[l]["pool_bd"], reads=[("wd", l, "pool_bd", 0)], writes=["pool_bd"])
            for kv in range(2):
                b = 64 * kv
                for li in range(32):
                    MM(psM[:, kv:kv + 1], w1kv[b:b + 64, li, :], poskv[b:b + 64, li:li + 1],
                       li == 0, li == 31, ["w1kv", "poskv"], ["psM"])
                V(lambda e, kv=kv: e.tensor_tensor(out=c1[:, kv:kv + 1], in0=psM[:, kv:kv + 1],
                                                   in1=small[:, 8 + kv:9 + kv], op=ALU.add),
                  ["psM", "small"], ["c1"])

            for s in range(NSEQ):
                r0 = s * T
                V(lambda e: e.memset(KCC[:], 0.0), [], ["KCC"])
                V(lambda e: e.memset(VCA[:, :, 0:64], 0.0), [], ["VCA"])
                V(lambda e: e.memset(RSUM[:], 0.0), [], ["RSUM"])
                V(lambda e: e.memset(PIN[:, :, 0:16], 0.0), [], ["PINh"])
                V(lambda e: e.memset(CU[:, :, 0:2], 0.0), [], ["CUh"])

                for g in range(NG):
                    t0 = g * G
                    tile0 = g * QI
                    S.dma(ropeC[:], cd["ropec"][:, t0:t0 + G], writes=["ropeC"])
                    S.dma(ropeS[:], cd["ropes"][:, t0:t0 + G], writes=["ropeS"])
                    for qi in range(QI):
                        rr = r0 + t0 + qi * 128
                        S.dma(xg[:, qi, :], xsrc[rr:rr + 128, :], reads=[("xd", l, s, g, qi)], writes=[("xg", qi)])
                    for qi in range(QI):
                        A(lambda e, qi=qi: e.activation(out=sq[:], in_=xg[:, qi, :], func=AF.Square),
                          [("xg", qi)], ["sq"])
                        V(lambda e, qi=qi: e.tensor_reduce(out=ss[:, qi:qi + 1], in_=sq[:], axis=mybir.AxisListType.X, op=ALU.add),
                          ["sq"], [("ss", qi)])
                        A(lambda e, qi=qi: e.activation(out=rstd[:, qi:qi + 1], in_=ss[:, qi:qi + 1], func=AF.Sqrt,
                                                        scale=1.0 / D, bias=epsb[:, 0:1]),
                          [("ss", qi), "epsb"], [("rstd", qi)])
                        V(lambda e, qi=qi: e.reciprocal(out=rstd[:, qi:qi + 1], in_=rstd[:, qi:qi + 1]),
                          [("rstd", qi)], [("rstd", qi)])
                        V(lambda e, qi=qi: e.tensor_scalar(out=xn[:], in0=xg[:, qi, :], scalar1=rstd[:, qi:qi + 1],
                                                           scalar2=None, op0=ALU.mult),
                          [("xg", qi), ("rstd", qi)], ["xn"])
                        for c in range(8):
                            MM(psP[:, c // 4, (c % 4) * 128:(c % 4) * 128 + 128], xn[:, c * 128:(c + 1) * 128], ident[:],
                               True, True, ["xn", "ident"], [("psP", c // 4)])
                        for c in range(8):
                            V(lambda e, c=c, qi=qi: e.tensor_scalar(
                                out=hT[:, c, qi * 128:(qi + 1) * 128],
                                in0=psP[:, c // 4, (c % 4) * 128:(c % 4) * 128 + 128],
                                scalar1=small[:, c:c + 1], scalar2=None, op0=ALU.mult),
                              [("psP", c // 4), "small"], [("hT", qi)])
                    hT_keys = [("hT", qi) for qi in range(QI)]

                    def fm_block(wt, wkey, blk, bank):
                        for kc in range(8):
                            MM(psP[:, bank, 0:G], wt[:, kc, blk * 128:(blk + 1) * 128], hT[:, kc, :],
                               kc == 0, kc == 7, hT_keys + [wkey], [("psP", bank)])

                    def unit_main(u, ncol=512):
                        return next_unit()

                    def rope_pair(wt, wkey, blk, dst_ap, dst_key):
                        fm_block(wt, wkey, blk, 0)
                        fm_block(wt, wkey, blk + 1, 1)
                        V(lambda e: e.tensor_tensor(out=rt1[:], in0=psP[:, 0, 0:G], in1=ropeC[:], op=ALU.mult),
                          [("psP", 0), "ropeC"], ["rt1"])
                        V(lambda e: e.tensor_tensor(out=rt2[:], in0=psP[:, 1, 0:G], in1=ropeS[:], op=ALU.mult),
                          [("psP", 1), "ropeS"], ["rt2"])
                        V(lambda e: e.tensor_tensor(out=dst_ap, in0=rt1[:], in1=rt2[:], op=ALU.add),
                          ["rt1", "rt2"], [dst_key])

                    def fm_copy(wt, wkey, blk, dst_ap, dst_key, func=AF.Copy):
                        bank = cnt["p"] % 2
                        cnt["p"] += 1
                        fm_block(wt, wkey, blk, bank)
                        A(lambda e: e.activation(out=dst_ap, in_=psP[:, bank, 0:G], func=func),
                          [("psP", bank)], [dst_key])

                    S.enabled = STAGE >= 2
                    wt, wk = unit_main(0)
                    rope_pair(wt, wk, 0, QN[0][:], "QN0")
                    rope_pair(wt, wk, 2, QN[1][:], "QN1")
                    S.enabled = STAGE >= 2.1
                    wt, wk = unit_main(1)
                    rope_pair(wt, wk, 0, KS2[:, t0:t0 + G], ("KS2", g))
                    rope_pair(wt, wk, 2, KW2[:, t0:t0 + G], ("KW2", g))
                    S.enabled = STAGE >= 2.2
                    wt, wk = unit_main(2)
                    fm_copy(wt, wk, 0, KCVC[:, t0:t0 + G], ("KCVC", g))
                    fm_copy(wt, wk, 1, FQ[0][:], "FQ0")
                    fm_copy(wt, wk, 2, FQ[1][:], "FQ1")
                    fm_copy(wt, wk, 3, FK[0][:, t0:t0 + G], ("FK0", g))
                    S.enabled = STAGE >= 2.3
                    wt, wk = unit_main(3)
                    fm_copy(wt, wk, 0, FK[1][:, t0:t0 + G], ("FK1", g))
                    for qi in range(QI):
                        bank = cnt["p"] % 2
                        cnt["p"] += 1
                        ti = tile0 + qi
                        for kc in range(8):
                            MM(psP[:, bank, 0:384], hT[:, kc, qi * 128:(qi + 1) * 128], wt[:, kc, 128:512],
                               kc == 0, kc == 7, [("hT", qi), wk], [("psP", bank)])
                        A(lambda e, bank=bank, ti=ti: e.activation(out=VS[:, ti, 0:64], in_=psP[:, bank, 0:64], func=AF.Copy),
                          [("psP", bank)], [("VS", ti)])
                        A(lambda e, bank=bank, ti=ti: e.activation(out=VW[:, ti, 0:64], in_=psP[:, bank, 64:128], func=AF.Copy),
                          [("psP", bank)], [("VW", ti)])
                        for hh in range(4):
                            A(lambda e, bank=bank, ti=ti, hh=hh: e.activation(
                                out=FV[:, ti, hh, 0:64], in_=psP[:, bank, 128 + 64 * hh:192 + 64 * hh], func=AF.Copy),
                              [("psP", bank)], [("FV", ti)])
                    S.enabled = STAGE >= 2.4
                    wt, wk = unit_main(4)
                    fm_copy(wt, wk, 0, PIN[:, 0, 16:16 + G], ("PIN", 0))
                    fm_copy(wt, wk, 1, PIN[:, 1, 16:16 + G], ("PIN", 1))
                    fm_copy(wt, wk, 2, CX[:, 0, :], ("CX", 0))
                    fm_copy(wt, wk, 3, CX[:, 1, :], ("CX", 1))
                    S.enabled = STAGE >= 2.5
                    wt, wk = unit_main(5)
                    fm_copy(wt, wk, 0, CB[:, 0, :], ("CB", 0))
                    fm_copy(wt, wk, 1, CB[:, 1, :], ("CB", 1))
                    for c in range(2):
                        bank = cnt["p"] % 2
                        cnt["p"] += 1
                        fm_block(wt, wk, 2 + c, bank)
                        V(lambda e, c=c, bank=bank: e.tensor_tensor(out=CU[:, c, 2:2 + G], in0=psP[:, bank, 0:G],
                                                                    in1=CX[:, c, :], op=ALU.mult),
                          [("psP", bank), ("CX", c)], [("CU", c)])
                    S.enabled = STAGE >= 2.6
                    wt, wk = unit_main(6)
                    for b in range(4):
                        fm_copy(wt, wk, b, GPC[:, b, :], ("GPC", b), func=AF.Silu)
                    S.enabled = STAGE >= 2.7
                    wt, wk = unit_main(7)
                    for qi in range(QI):
                        bank = cnt["p"] % 2
                        cnt["p"] += 1
                        for kc in range(8):
                            MM(psP[:, bank, :], hT[:, kc, qi * 128:(qi + 1) * 128], wt[:, kc, :],
                               kc == 0, kc == 7, [("hT", qi), wk], [("psP", bank)])
                        A(lambda e, bank=bank, qi=qi: e.activation(out=GNF[:, qi, :], in_=psP[:, bank, :], func=AF.Silu),
                          [("psP", bank)], [("GNF", qi)])
                    S.enabled = STAGE >= 2.8
                    wt, wk = unit_main(8, ncol=16)
                    for qi in range(QI):
                        for kc in range(8):
                            MM(psM[:, qi * 16:qi * 16 + 16], hT[:, kc, qi * 128:(qi + 1) * 128], wt[:, kc, 0:16],
                               kc == 0, kc == 7, [("hT", qi), wk], ["psM"])
                    A(lambda e: e.activation(out=U8s[:, 0:QI * 16], in_=psM[:, 0:QI * 16], func=AF.Copy), ["psM"], ["U8s"])
                    for qi in range(QI):
                        A(lambda e, qi=qi: e.activation(out=NGt[:, qi, :], in_=U8s[:, qi * 16:qi * 16 + 12], func=AF.Sigmoid),
                          ["U8s"], ["NGt"])
                    for qi in range(QI):
                        V(lambda e, qi=qi: e.tensor_tensor(out=LFt[:, qi, :], in0=U8s[:, qi * 16 + 12:qi * 16 + 16],
                                                           in1=fbbc[:, l * 4:l * 4 + 4], op=ALU.add),
                          ["U8s", "fbbc"], ["LFt"])
                    A(lambda e: e.activation(out=LFt[:], in_=LFt[:], func=AF.Exp, scale=-1.0), ["LFt"], ["LFt"])
                    A(lambda e: e.activation(out=LFt[:], in_=LFt[:], func=AF.Ln, bias=oneb[:, 0:1]), ["LFt", "oneb"], ["LFt"])
                    V(lambda e: e.tensor_scalar(out=LF[:], in0=LFt[:], scalar1=-1.0, scalar2=None, op0=ALU.mult),
                      ["LFt"], ["LF"])
                    S.enabled = STAGE >= 3
                    for qi in range(QI):
                        ti = tile0 + qi
                        MM(psM[:, 0:4], trif[:], LF[:, qi, :], True, False, ["trif", "LF"], ["psM"])
                        MM(psM[:, 0:4], onesf[:], RSUM[:], False, True, ["onesf", "RSUM"], ["psM"])
                        V(lambda e, qi=qi: e.tensor_tensor(out=RSUM[:], in0=RSUM[:], in1=LF[:, qi, :], op=ALU.add),
                          ["RSUM", "LF"], ["RSUM"])
                        MM(psM[:, 4:8], onesf[:], RSUM[:], True, True, ["onesf", "RSUM"], ["psM"])
                        V(lambda e, ti=ti: e.tensor_copy(out=CKR[:, ti, :], in_=psM[:, 0:8]), ["psM"], [("CKR", ti)])
                    S.enabled = STAGE >= 4
                    n0 = max(0, (t0 // 16) - 1)
                    n1 = min(NC - 1, (t0 + G) // 16 - 2)
                    nn = n1 - n0 + 1
                    kc_keys = [("KCVC", gg) for gg in range(g + 1)]
                    for kv in range(2):
                        b = 64 * kv
                        for li in range(32):
                            a0 = 16 * n0 + li
                            MM(psM[:, 0:nn], w1kv[b:b + 64, li, :], KCVC[b:b + 64, a0:a0 + 16 * (nn - 1) + 1:16],
                               li == 0, li == 31, ["w1kv"] + kc_keys, ["psM"])
                        A(lambda e, kv=kv: e.activation(out=hid[kv][:, 0:nn], in_=psM[:, 0:nn], func=AF.Silu,
                                                        bias=c1[:, kv:kv + 1]),
                          ["psM", "c1"], [("hid", kv)])
                    MM(psM[:, 0:nn], w2k[:, 0, :], hid[0][:, 0:nn], True, True, ["w2k", ("hid", 0)], ["psM"])
                    MM(psM[:, 32:32 + nn], w2k[:, 1, :], hid[0][:, 0:nn], True, True, ["w2k", ("hid", 0)], ["psM"])
                    V(lambda e: e.scalar_tensor_tensor(out=kt1[:, 0:nn], in0=psM[:, 0:nn], scalar=small[:, 10:11],
                                                       in1=cmpc[:, n0:n0 + nn], op0=ALU.add, op1=ALU.mult),
                      ["psM", "small", "cmpc"], ["kt1"])
                    V(lambda e: e.scalar_tensor_tensor(out=kt2[:, 0:nn], in0=psM[:, 32:32 + nn], scalar=small[:, 11:12],
                                                       in1=cmps[:, n0:n0 + nn], op0=ALU.add, op1=ALU.mult),
                      ["psM", "small", "cmps"], ["kt2"])
                    V(lambda e: e.tensor_tensor(out=KCC[:, n0:n0 + nn], in0=kt1[:, 0:nn], in1=kt2[:, 0:nn], op=ALU.add),
                      ["kt1", "kt2"], ["KCC"])
                    MM(psM[0:nn, 64:128], hid[1][:, 0:nn], w2v[:], True, False, [("hid", 1), "w2v"], ["psM"])
                    MM(psM[0:nn, 64:128], onesb[0:1, 0:nn], b2v[0:1, :], False, True, ["onesb", "b2v"], ["psM"])
                    V(lambda e: e.tensor_copy(out=vstage[0:nn, :], in_=psM[0:nn, 64:128]), ["psM"], ["vstage"])
                    i0 = 0
                    while i0 < nn:
                        na = n0 + i0
                        j, p0 = na // 128, na % 128
                        cntr = min(nn - i0, 128 - p0)
                        S.dma(VCA[p0:p0 + cntr, j, 0:64], vstage[i0:i0 + cntr, :], reads=["vstage"], writes=["VCA"])
                        i0 += cntr
                    S.enabled = STAGE >= 5
                    jvalid = [j for j in range(NCT) if 16 * (128 * j) + 31 <= t0 + G - 1]
                    for j in jvalid:
                        S.dma(cm[j][:], cd["cmask"][:, t0 - 2048 * j:t0 - 2048 * j + G], writes=[("cm", j)])
                    W = 129
                    for h in range(4):
                        pr, pb = h // 2, 64 * (h % 2)
                        akeys = slot_keys(QI, W)
                        for ak in akeys:
                            bk = ak[1]
                            V(lambda e, bk=bk: e.memset(acc[:, bk, :], 0.0), [], [ak])
                        for j in jvalid:
                            bS = cnt["s"] % 2
                            cnt["s"] += 1
                            MM(psS[:, bS, 0:G], KCC[pb:pb + 64, j * 128:(j + 1) * 128], QN[pr][pb:pb + 64, :],
                               True, True, ["KCC", "QN%d" % pr], [("psS", bS)])
                            be = cnt["e"] % 2
                            cnt["e"] += 1
                            A(lambda e, bS=bS, be=be: e.activation(out=ET[be][:], in_=psS[:, bS, 0:G], func=AF.Exp, scale=0.125),
                              [("psS", bS)], [("ET", be)])
                            V(lambda e, be=be, j=j: e.tensor_tensor(out=PTC[j][:], in0=ET[be][:], in1=cm[j][:], op=ALU.mult),
                              [("ET", be), ("cm", j)], [("PTC", j)])
                        for qi in range(QI):
                            for j in jvalid:
                                MM(slot(qi, W), PTC[j][:, qi * 128:(qi + 1) * 128], VCA[:, j, :], False, False,
                                   [("PTC", j), "VCA", "VCA1", "VCAo"], akeys, skip=True)
                        for qi in range(QI):
                            sl = slot(qi, W)
                            V(lambda e, sl=sl: e.tensor_scalar(out=den[:], in0=sl[:, 64:65], scalar1=1e-30, scalar2=None, op0=ALU.max),
                              akeys, ["den"])
                            V(lambda e: e.reciprocal(out=rden[:], in_=den[:]), ["den"], ["rden"])
                            if h == 0:
                                V(lambda e, sl=sl, qi=qi: e.tensor_scalar(out=IMP[:, qi, :], in0=sl[:, 65:129], scalar1=rden[:, 0:1],
                                                                          scalar2=None, op0=ALU.mult),
                                  akeys + ["rden"], [("IMP", qi)])
                            else:
                                V(lambda e, sl=sl, qi=qi: e.scalar_tensor_tensor(out=IMP[:, qi, :], in0=sl[:, 65:129], scalar=rden[:, 0:1],
                                                                                 in1=IMP[:, qi, :], op0=ALU.mult, op1=ALU.add),
                                  akeys + ["rden", ("IMP", qi)], [("IMP", qi)])
                            V(lambda e, qi=qi, h=h: e.tensor_tensor(out=scl[:], in0=rden[:], in1=NGt[:, qi, h:h + 1], op=ALU.mult),
                              ["rden", "NGt"], ["scl"])
                            V(lambda e, sl=sl, qi=qi, h=h: e.tensor_scalar(out=ONSA[:, qi, h * 64:(h + 1) * 64], in0=sl[:, 0:64],
                                                                           scalar1=scl[:, 0:1], scalar2=None, op0=ALU.mult),
                              akeys + ["scl"], [("ONSA", qi)])
                    for qi in range(QI):
                        S.dma(addt[:], cd["addtab"][tile0 + qi], writes=["addt"])
                        V(lambda e, qi=qi: e.tensor_tensor(out=IMPM[:], in0=IMP[:, qi, :], in1=addt[:], op=ALU.add),
                          [("IMP", qi), "addt"], ["IMPM"])
                        V(lambda e: e.max(out=m8a[:], in_=IMPM[:]), ["IMPM"], ["m8a"])
                        V(lambda e: e.match_replace(out=IMPM2[:], in_to_replace=m8a[:], in_values=IMPM[:], imm_value=-3.0e38),
                          ["IMPM", "m8a"], ["IMPM2"])
                        V(lambda e: e.memset(fence[:], 0.0), ["IMPM2"], ["IMPM2", "fence"])
                        V(lambda e: e.max(out=m8b[:], in_=IMPM2[:]), ["IMPM2"], ["m8b"])
                        V(lambda e: e.tensor_scalar(out=sel[:], in0=IMPM[:], scalar1=m8b[:, 7:8], scalar2=None, op0=ALU.is_ge),
                          ["IMPM", "m8b"], ["sel"])
                        MM(psM[0:64, qi * 128:(qi + 1) * 128], sel[:], ident[:], True, True, ["sel", "ident"], ["psM"])
                    A(lambda e: e.activation(out=selT[:], in_=psM[0:64, 0:G], func=AF.Copy), ["psM"], ["selT"])

                    S.enabled = STAGE >= 6
                    W = 65

                    def acc_reset():
                        ks = slot_keys(4 * QI, W)
                        for ak in ks:
                            bk = ak[1]
                            V(lambda e, bk=bk: e.memset(acc[:, bk, :], 0.0), [], [ak])
                        return ks

                    def evac(akeys, dst, gate_off, first):
                        ns = 4 * QI
                        per = 512 // W
                        for b0 in range(0, ns, per):
                            n = min(per, ns - b0)
                            V(lambda e, b0=b0, n=n: e.tensor_scalar(out=den8[:, b0:b0 + n], in0=acc[:, b0 // per, 64:64 + W * (n - 1) + 1:W],
                                                                    scalar1=1e-30, scalar2=None, op0=ALU.max),
                              akeys, ["den8"])
                        V(lambda e: e.reciprocal(out=rden8[:, 0:ns], in_=den8[:, 0:ns]), ["den8"], ["rden8"])
                        if gate_off is not None:
                            for qi in range(QI):
                                V(lambda e, qi=qi: e.tensor_tensor(out=scl8[:, qi:ns:QI], in0=rden8[:, qi:ns:QI],
                                                                   in1=NGt[:, qi, gate_off:gate_off + 4], op=ALU.mult),
                                  ["rden8", "NGt"], ["scl8"])
                        for h in range(4):
                            for qi in range(QI):
                                i = h * QI + qi
                                sl = slot(i, W)
                                if gate_off is None:
                                    V(lambda e, sl=sl, qi=qi, h=h, i=i: e.tensor_scalar(out=dst[:, qi, h * 64:(h + 1) * 64], in0=sl[:, 0:64],
                                                                                        scalar1=rden8[:, i:i + 1], scalar2=None, op0=ALU.mult),
                                      akeys + ["rden8"], [("OFOX", qi)])
                                else:
                                    V(lambda e, sl=sl, qi=qi, h=h, i=i: e.scalar_tensor_tensor(
                                        out=dst[:, qi, h * 64:(h + 1) * 64], in0=sl[:, 0:64], scalar=scl8[:, i:i + 1],
                                        in1=dst[:, qi, h * 64:(h + 1) * 64], op0=ALU.mult, op1=ALU.add),
                                      akeys + ["scl8", ("ONSA", qi)], [("ONSA", qi)])

                    pend = []

                    def flush():
                        for f_ in pend:
                            f_()
                        del pend[:]

                    def defer_mm(o_, l_, r_, rd_, ak_):
                        pend.append(lambda: MM(o_, l_, r_, False, False, rd_, ak_, skip=True))

                    akeys = acc_reset()
                    nkt = tile0 + QI
                    for kt in range(nkt):
                        di = kt - tile0
                        bm = cnt["m"] % 2
                        cnt["m"] += 1
                        MM(psM[:, 0:G], efull[:, kt * 128:(kt + 1) * 128], selT[:], True, True, ["efull", "selT"], ["psM"])
                        A(lambda e, bm=bm: e.activation(out=MT[bm][:], in_=psM[:, 0:G], func=AF.Copy), ["psM"], [("MT", bm)])
                        if di >= 0:
                            V(lambda e, bm=bm, di=di: e.tensor_tensor(out=MT[bm][:, di * 128:(di + 1) * 128],
                                                                      in0=MT[bm][:, di * 128:(di + 1) * 128], in1=tri[:], op=ALU.mult),
                              [("MT", bm), "tri"], [("MT", bm)])
                        for h in range(4):
                            pr, pb = h // 2, 64 * (h % 2)
                            bS = cnt["s"] % 2
                            cnt["s"] += 1
                            MM(psS[:, bS, 0:G], KS2[pb:pb + 64, kt * 128:(kt + 1) * 128], QN[pr][pb:pb + 64, :],
                               True, True, [("KS2", kt // QI), "QN%d" % pr], [("psS", bS)])
                            flush()
                            be = cnt["e"] % 2
                            cnt["e"] += 1
                            A(lambda e, bS=bS, be=be: e.activation(out=ET[be][:], in_=psS[:, bS, 0:G], func=AF.Exp, scale=0.125),
                              [("psS", bS)], [("ET", be)])
                            bp = cnt["t"] % 2
                            cnt["t"] += 1
                            V(lambda e, be=be, bm=bm, bp=bp: e.tensor_tensor(out=PTt[bp][:], in0=ET[be][:], in1=MT[bm][:], op=ALU.mult),
                              [("ET", be), ("MT", bm)], [("PTt", bp)])
                            for qi in range(max(0, di), QI):
                                defer_mm(slot(h * QI + qi, W), PTt[bp][:, qi * 128:(qi + 1) * 128], VS[:, kt, :],
                                         [("PTt", bp), ("VS", kt), "VS1"], akeys)
                    flush()
                    evac(akeys, ONSA, 4, False)

                    S.enabled = STAGE >= 7
                    akeys = acc_reset()
                    for kt in range(max(0, tile0 - 4), nkt):
                        rel = [(qi, tile0 + qi - kt) for qi in range(QI)]
                        qis = [qi for qi, r in rel if 0 <= r <= 4]
                        if not qis:
                            continue
                        for h in range(4):
                            pr, pb = h // 2, 64 * (h % 2)
                            bS = cnt["s"] % 2
                            cnt["s"] += 1
                            MM(psS[:, bS, 0:G], KW2[pb:pb + 64, kt * 128:(kt + 1) * 128], QN[pr][pb:pb + 64, :],
                               True, True, [("KW2", kt // QI), "QN%d" % pr], [("psS", bS)])
                            flush()
                            be = cnt["e"] % 2
                            cnt["e"] += 1
                            A(lambda e, bS=bS, be=be: e.activation(out=ET[be][:], in_=psS[:, bS, 0:G], func=AF.Exp, scale=0.125),
                              [("psS", bS)], [("ET", be)])
                            for qi, r in rel:
                                if r == 0 or r == 4:
                                    mk = tri if r == 0 else ntri
                                    V(lambda e, be=be, qi=qi, mk=mk: e.tensor_tensor(
                                        out=ET[be][:, qi * 128:(qi + 1) * 128], in0=ET[be][:, qi * 128:(qi + 1) * 128],
                                        in1=mk[:], op=ALU.mult),
                                      [("ET", be), "tri", "ntri"], [("ET", be)])
                            for qi in qis:
                                defer_mm(slot(h * QI + qi, W), ET[be][:, qi * 128:(qi + 1) * 128], VW[:, kt, :],
                                         [("ET", be), ("VW", kt), "VW1"], akeys)
                    flush()
                    evac(akeys, ONSA, 8, False)

                    S.enabled = STAGE >= 8
                    akeys = acc_reset()
                    for h in range(4):
                        for qi in range(QI):
                            nk = tile0 + qi + 1
                            V(lambda e, h=h, qi=qi, nk=nk: e.tensor_scalar(
                                out=BT[:, h, qi, 0:nk], in0=CKR[:, 0:nk, h], scalar1=-1.0,
                                scalar2=CKR[:, tile0 + qi, 4 + h:5 + h], op0=ALU.mult, op1=ALU.add),
                              [("CKR", ti) for ti in range(nk)], ["BT"])
                    for kt in range(nkt):
                        di = kt - tile0
                        for h in range(4):
                            pr, pb = h // 2, 64 * (h % 2)
                            bS = cnt["s"] % 2
                            cnt["s"] += 1
                            MM(psS[:, bS, 0:G], FK[pr][pb:pb + 64, kt * 128:(kt + 1) * 128], FQ[pr][pb:pb + 64, :],
                               True, True, [("FK%d" % pr, kt // QI), "FQ%d" % pr], [("psS", bS)])
                            flush()
                            be = cnt["e"] % 2
                            cnt["e"] += 1
                            for qi in range(max(0, di), QI):
                                A(lambda e, bS=bS, be=be, qi=qi, h=h, kt=kt: e.activation(
                                    out=ET[be][:, qi * 128:(qi + 1) * 128], in_=psS[:, bS, qi * 128:(qi + 1) * 128],
                                    func=AF.Exp, scale=0.125, bias=BT[:, h, qi, kt:kt + 1]),
                                  [("psS", bS), "BT"], [("ET", be)])
                            if di >= 0:
                                V(lambda e, be=be, di=di: e.tensor_tensor(out=ET[be][:, di * 128:(di + 1) * 128],
                                                                          in0=ET[be][:, di * 128:(di + 1) * 128], in1=tri[:], op=ALU.mult),
                                  [("ET", be), "tri"], [("ET", be)])
                            for qi in range(max(0, di), QI):
                                defer_mm(slot(h * QI + qi, W), ET[be][:, qi * 128:(qi + 1) * 128], FV[:, kt, h, :],
                                         [("ET", be), ("FV", kt), "FV1"], akeys)
                    flush()
                    evac(akeys, OFOX, None, True)

                    S.enabled = STAGE >= 9
                    for (src, skey, goff, ch0) in ((ONSA, "ONSA", 0, 0), (OFOX, "OFOX", 256, 6)):
                        for c in range(2):
                            for qi in range(QI):
                                V(lambda e, src=src, qi=qi, goff=goff, c=c: e.tensor_tensor(
                                    out=OG[:, 0:128], in0=src[:, qi, c * 128:(c + 1) * 128],
                                    in1=GNF[:, qi, goff + c * 128:goff + (c + 1) * 128], op=ALU.mult),
                                  [(skey, qi), ("GNF", qi)], ["OG"])
                                MM(psM[:, qi * 128:(qi + 1) * 128], OG[:, 0:128], ident[:], True, True, ["OG", "ident"], ["psM"])
                            A(lambda e, ch=ch0 + c: e.activation(out=ogT[:, ch, :], in_=psM[:, 0:G], func=AF.Copy),
                              ["psM"], [("ogT", ch0 + c)])

                    S.enabled = STAGE >= 10
                    GE = 16 + G
                    pin_keys = [("PIN", 0), ("PIN", 1), "PINh"]
                    V(lambda e: e.tensor_tensor(out=PA[:, :, 1:GE], in0=PIN[:, :, 1:GE], in1=PIN[:, :, 0:GE - 1], op=ALU.add),
                      pin_keys, ["PA"])

                    def pool_out(src, ch, plo):
                        V(lambda e: e.scalar_tensor_tensor(out=PL[plo:plo + 64, ch, :], in0=src[plo:plo + 64, ch, 16:GE],
                                                           scalar=invw[plo:plo + 64, ch:ch + 1], in1=PIN[plo:plo + 64, ch, 16:GE],
                                                           op0=ALU.mult, op1=ALU.subtract),
                          ["PA", "PB", "invw"] + pin_keys, [("PL", ch, plo)])
                        if g == 0:
                            V(lambda e: e.tensor_tensor(out=PT16[plo:plo + 64, :], in0=src[plo:plo + 64, ch, 16:32],
                                                        in1=invc[plo:plo + 64, ch, :], op=ALU.mult),
                              ["PA", "PB", "invc"], ["PT16"])
                            V(lambda e: e.tensor_tensor(out=PL[plo:plo + 64, ch, 0:16], in0=PT16[plo:plo + 64, :],
                                                        in1=PIN[plo:plo + 64, ch, 16:32], op=ALU.subtract),
                              ["PT16"] + pin_keys, [("PL", ch, plo)])

                    pool_out(PA, 0, 0)
                    V(lambda e: e.tensor_tensor(out=PB[:, :, 3:GE], in0=PA[:, :, 3:GE], in1=PA[:, :, 1:GE - 2], op=ALU.add),
                      ["PA"], ["PB"])
                    pool_out(PB, 0, 64)
                    V(lambda e: e.tensor_tensor(out=PA[:, 1, 7:GE], in0=PB[:, 1, 7:GE], in1=PB[:, 1, 3:GE - 4], op=ALU.add),
                      ["PB", ("PL", 0, 0)], ["PA"])
                    pool_out(PA, 1, 0)
                    V(lambda e: e.tensor_tensor(out=PB[64:128, 1, 15:GE], in0=PA[64:128, 1, 15:GE], in1=PA[64:128, 1, 7:GE - 8], op=ALU.add),
                      ["PA", ("PL", 0, 64)], ["PB"])
                    pool_out(PB, 1, 64)
                    V(lambda e: e.tensor_copy(out=PIN[:, :, 0:16], in_=PIN[:, :, G:G + 16]),
                      pin_keys + [("PL", 0, 0), ("PL", 0, 64), ("PL", 1, 0), ("PL", 1, 64)], ["PINh"])
                    for c in range(2):
                        bank = cnt["p"] % 2
                        cnt["p"] += 1
                        MM(psP[:, bank, 0:G], pool_bd[:, c, :], PL[:, c, :], True, True,
                           ["pool_bd", ("PL", c, 0), ("PL", c, 64)], [("psP", bank)])
                        V(lambda e, c=c, bank=bank: e.scalar_tensor_tensor(out=ogT[:, 2 + c, :], in0=psP[:, bank, 0:G],
                                                                           scalar=small[:, 12 + c:13 + c], in1=GPC[:, c, :],
                                                                           op0=ALU.mult, op1=ALU.mult),
                          [("psP", bank), "small", ("GPC", c)], [("ogT", 2 + c)])
                    S.enabled = STAGE >= 11
                    for c in range(2):
                        cuk = [("CU", c), "CUh"]
                        V(lambda e, c=c: e.tensor_scalar(out=CY[:], in0=CU[:, c, 0:G], scalar1=small[:, 14 + 3 * c:15 + 3 * c],
                                                         scalar2=None, op0=ALU.mult), cuk + ["small"], ["CY"])
                        V(lambda e, c=c: e.scalar_tensor_tensor(out=CY[:], in0=CU[:, c, 1:1 + G], scalar=small[:, 15 + 3 * c:16 + 3 * c],
                                                                in1=CY[:], op0=ALU.mult, op1=ALU.add), cuk + ["small", "CY"], ["CY"])
                        V(lambda e, c=c: e.scalar_tensor_tensor(out=CY[:], in0=CU[:, c, 2:2 + G], scalar=small[:, 16 + 3 * c:17 + 3 * c],
                                                                in1=CY[:], op0=ALU.mult, op1=ALU.add), cuk + ["small", "CY"], ["CY"])
                        V(lambda e, c=c: e.tensor_tensor(out=CY[:], in0=CY[:], in1=CB[:, c, :], op=ALU.mult), ["CY", ("CB", c)], ["CY"])
                        V(lambda e, c=c: e.tensor_tensor(out=ogT[:, 4 + c, :], in0=CY[:], in1=GPC[:, 2 + c, :], op=ALU.mult),
                          ["CY", ("GPC", 2 + c)], [("ogT", 4 + c)])
                    V(lambda e: e.tensor_copy(out=CU[:, :, 0:2], in_=CU[:, :, G:G + 2]), [("CU", 0), ("CU", 1), "CUh", "CY"], ["CUh"])

                    S.enabled = STAGE >= 12
                    for c in range(8):
                        i, half = c // 2, c % 2
                        wt, wk = next_unit()
                        for qi in range(QI):
                            for kc in range(8):
                                MM(psP[:, 0, :], hT[:, kc, qi * 128:(qi + 1) * 128], wt[:, kc, :],
                                   kc == 0, kc == 7, [("hT", qi), wk], [("psP", 0)])
                            A(lambda e: e.activation(out=SM[:], in_=psP[:, 0, :], func=AF.Sigmoid), [("psP", 0)], ["SM"])
                            for fc in range(2):
                                MM(psP[:, 1, :], ogT[:, 2 * i + fc, qi * 128:(qi + 1) * 128], wt[:, 8 + fc, :],
                                   fc == 0, fc == 1, [("ogT", 2 * i + fc), wk], [("psP", 1)])
                            dsta = ACCM[:, qi, half * 512:(half + 1) * 512]
                            if i == 0:
                                V(lambda e, dsta=dsta: e.tensor_tensor(out=dsta, in0=psP[:, 1, :], in1=SM[:], op=ALU.mult),
                                  [("psP", 1), "SM"], [("ACCM", qi, half)])
                            else:
                                V(lambda e: e.tensor_tensor(out=TMPt[:], in0=psP[:, 1, :], in1=SM[:], op=ALU.mult),
                                  [("psP", 1), "SM"], ["TMPt"])
                                V(lambda e, dsta=dsta: e.tensor_tensor(out=dsta, in0=dsta, in1=TMPt[:], op=ALU.add),
                                  ["TMPt", ("ACCM", qi, half)], [("ACCM", qi, half)])
                    S.enabled = STAGE >= 13
                    for qi in range(QI):
                        V(lambda e, qi=qi: e.tensor_copy(out=ACCB[:], in_=ACCM[:, qi, :]), [("ACCM", qi, 0), ("ACCM", qi, 1)], ["ACCB"])
                        for c in range(8):
                            MM(psP[:, c // 4, (c % 4) * 128:(c % 4) * 128 + 128], ACCB[:, c * 128:(c + 1) * 128], ident[:],
                               True, True, ["ACCB", "ident"], [("psP", c // 4)])
                        for c in range(8):
                            A(lambda e, c=c, qi=qi: e.activation(
                                out=hT[:, c, qi * 128:(qi + 1) * 128],
                                in_=psP[:, c // 4, (c % 4) * 128:(c % 4) * 128 + 128], func=AF.Copy),
                              [("psP", c // 4)], [("hT", qi)])
                    for half in range(2):
                        wt, wk = next_unit()
                        for qi in range(QI):
                            bank = cnt["p"] % 2
                            cnt["p"] += 1
                            for fc in range(8):
                                MM(psP[:, bank, :], hT[:, fc, qi * 128:(qi + 1) * 128], wt[:, fc, :],
                                   fc == 0, fc == 7, [("hT", qi), wk], [("psP", bank)])
                            V(lambda e, bank=bank, qi=qi, half=half: e.tensor_tensor(
                                out=xg[:, qi, half * 512:(half + 1) * 512], in0=psP[:, bank, :],
                                in1=xg[:, qi, half * 512:(half + 1) * 512], op=ALU.add),
                              [("psP", bank), ("xg", qi)], [("xg", qi)])
                    S.enabled = True
                    if DEBUG and l == 0:
                        for qi in range(QI):
                            rr = r0 + t0 + qi * 128
                            S.dma(dbg[rr:rr + 128, 0:256], ONSA[:, qi, :], reads=[("ONSA", qi)], writes=[("dbg", g, qi)])
                            S.dma(dbg[rr:rr + 128, 256:320], IMP[:, qi, :], reads=[("IMP", qi)], writes=[("dbg2", g, qi)])
                            S.dma(dbg[rr:rr + 128, 320:328], m8a[:], reads=["m8a"], writes=[("dbg3", g, qi)])
                            S.dma(dbg[rr:rr + 128, 328:336], m8b[:], reads=["m8b"], writes=[("dbg4", g, qi)])
                            S.dma(dbg[rr:rr + 128, 336:400], IMPM2[:], reads=["IMPM2"], writes=[("dbg5", g, qi)])
                            S.dma(dbg[rr:rr + 128, 400:464], sel[:], reads=["sel"], writes=[("dbg6", g, qi)], eng="gpsimd")
                    for qi in range(QI):
                        rr = r0 + t0 + qi * 128
                        if not lastl:
                            S.dma(xdst[rr:rr + 128, :], xg[:, qi, :], reads=[("xg", qi)], writes=[("xd", l + 1, s, g, qi)])
                        else:
                            A(lambda e, qi=qi: e.activation(out=sq[:], in_=xg[:, qi, :], func=AF.Square),
                              [("xg", qi)], ["sq"])
                            V(lambda e, qi=qi: e.tensor_reduce(out=ss[:, qi:qi + 1], in_=sq[:], axis=mybir.AxisListType.X, op=ALU.add),
                              ["sq"], [("ss", qi)])
                            A(lambda e, qi=qi: e.activation(out=rstd[:, qi:qi + 1], in_=ss[:, qi:qi + 1], func=AF.Sqrt,
                                                            scale=1.0 / D, bias=epsb[:, 0:1]), [("ss", qi), "epsb"], [("rstd", qi)])
                            V(lambda e, qi=qi: e.reciprocal(out=rstd[:, qi:qi + 1], in_=rstd[:, qi:qi + 1]),
                              [("rstd", qi)], [("rstd", qi)])
                            V(lambda e, qi=qi: e.scalar_tensor_tensor(out=YO[:], in0=xg[:, qi, :], scalar=rstd[:, qi:qi + 1],
                                                                      in1=fgbc[:], op0=ALU.mult, op1=ALU.mult),
                              [("xg", qi), ("rstd", qi), "fgbc"], ["YO"])
                            S.dma(xdst[rr:rr + 128, :], YO[:], reads=["YO"], writes=[("yd", s, g, qi)])
        S.finalize()
    return nc


_CACHE = {}


def run_model(inp, T, DEPTH, n_cores, NSEQ, QI):
    consts = prep_consts(T, DEPTH, inp)
    layers = [prep_layer(inp, l) for l in range(DEPTH)]
    key = (T, DEPTH, NSEQ, QI)
    if key not in _CACHE:
        _CACHE[key] = build_program(T, DEPTH, NSEQ, QI, {k: v.shape for k, v in consts.items()},
                                    {k: layers[0][k].shape for k in LAYER_KEYS})
    nc = _CACHE[key]
    x = np.asarray(inp["x"], np.float32)
    B = x.shape[0]
    assert B == n_cores * NSEQ
    in_maps = []
    for c in range(n_cores):
        m = {"x": np.ascontiguousarray(x[c * NSEQ:(c + 1) * NSEQ].reshape(NSEQ * T, D))}
        for k, v in consts.items():
            m["c_" + k] = v
        for l in range(DEPTH):
            for k in LAYER_KEYS:
                m["l%d_%s" % (l, k)] = layers[l][k]
        in_maps.append(m)
    res = run_bass_kernel_spmd(nc, in_maps, core_ids=list(range(n_cores)), **RUN_KW)
    global LAST_RES
    LAST_RES = res
    out = np.stack([res.results[c]["y"].reshape(NSEQ, T, D) for c in range(n_cores)], 0)
    if DEBUG:
        global LAST_DBG
        LAST_DBG = [res.results[c]["dbg"] for c in range(n_cores)]
    return out.reshape(B, T, D).astype(np.float32)


def kernel(**inputs):
    return run_model(inputs, 4096, 2, N_CORES, 2, 2)
```

```python
import contextlib
import numpy as np
import concourse.bass as bass
import concourse.mybir as mybir
from concourse.bass_utils import run_bass_kernel_spmd

F32 = mybir.dt.float32
BF16 = mybir.dt.bfloat16
AF = mybir.ActivationFunctionType
ALU = mybir.AluOpType

D = 1024
HD = 64
N_CORES = 8
STAGE = 99
RUN_KW = {}
LAST_RES = None
DEBUG = False

ENGS = ("tensor", "vector", "scalar", "gpsimd", "sync")
N_DMA_SEMS = 40


class Op:
    __slots__ = ("eng", "fn", "deps", "idx", "dma", "marked", "inc", "dsem", "dval")

    def __init__(self, eng, fn, dma):
        self.eng = eng
        self.fn = fn
        self.deps = []
        self.dma = dma
        self.marked = False
        self.inc = 0
        self.dsem = None
        self.dval = 0


class _Rec:
    def __getattr__(self, name):
        def f(*a, **kw):
            return (name, a, kw)
        return f


_REC = _Rec()


class Sched:
    def __init__(self, nc):
        self.nc = nc
        self.ops = {e: [] for e in ENGS}
        self.last_writer = {}
        self.readers = {}
        self.n = 0
        self.enabled = True

    def add(self, eng, fn, reads=(), writes=(), dma=False):
        if not self.enabled:
            return None
        call = fn(_REC)
        op = Op(eng, call, dma)
        op.idx = self.n
        self.n += 1
        deps = {}
        for k in reads:
            w = self.last_writer.get(k)
            if w is not None:
                deps[w.idx] = w
        for k in writes:
            w = self.last_writer.get(k)
            if w is not None:
                deps[w.idx] = w
            lastr = {}
            for r in self.readers.get(k, ()):
                if r.dma:
                    deps[r.idx] = r
                    continue
                if r.eng == eng and not dma:
                    continue
                lastr[r.eng] = r
            for r in lastr.values():
                deps[r.idx] = r
        for d in deps.values():
            if d.eng == eng and not d.dma and not dma and eng == "tensor":
                continue
            op.deps.append(d)
            d.marked = True
        for k in reads:
            self.readers.setdefault(k, []).append(op)
        for k in writes:
            self.last_writer[k] = op
            self.readers[k] = []
        self.ops[eng].append(op)
        return op

    def dma(self, out, in_, reads=(), writes=(), eng="sync"):
        return self.add(eng, lambda e: e.dma_start(out=out, in_=in_), reads, writes, dma=True)

    def finalize(self, final_wait_eng="sync"):
        nc = self.nc
        for e in ENGS:
            c = 0
            for op in self.ops[e]:
                if not op.dma and op.marked:
                    c += 1
                    op.inc = c
        with contextlib.ExitStack() as st:
            esem = {e: st.enter_context(nc.semaphore("s_" + e)) for e in ENGS}
            dsems = [st.enter_context(nc.semaphore("d%d" % i)) for i in range(N_DMA_SEMS)]
            dtot = [0] * N_DMA_SEMS
            all_dma = sorted([op for e in ENGS for op in self.ops[e] if op.dma], key=lambda o: o.idx)
            for i, op in enumerate(all_dma):
                s = i % N_DMA_SEMS
                dtot[s] += 16
                op.dsem = s
                op.dval = dtot[s]
            block = st.enter_context(nc.Block())
            ops = self.ops

            def emit_engine(ename, eng):
                waited = {}

                def wait(key, sem, val):
                    if waited.get(key, 0) >= val:
                        return
                    waited[key] = val
                    eng.wait_ge(sem, val)

                for op in ops[ename]:
                    for d in op.deps:
                        if d.dma:
                            wait(("d", d.dsem), dsems[d.dsem], d.dval)
                        else:
                            wait(("e", d.eng), esem[d.eng], d.inc)
                    if op.dma:
                        if op.dval > 16:
                            wait(("d", op.dsem), dsems[op.dsem], op.dval - 16)
                        ins = getattr(eng, op.fn[0])(*op.fn[1], **op.fn[2])
                        ins.then_inc(dsems[op.dsem], 16)
                    else:
                        ins = getattr(eng, op.fn[0])(*op.fn[1], **op.fn[2])
                        if op.marked:
                            ins.then_inc(esem[ename], 1)
                if ename == final_wait_eng:
                    for s in range(N_DMA_SEMS):
                        if dtot[s] > 0:
                            wait(("d", s), dsems[s], dtot[s])

            @block.sync
            def _(eng):
                emit_engine("sync", eng)

            @block.tensor
            def _(eng):
                emit_engine("tensor", eng)

            @block.vector
            def _(eng):
                emit_engine("vector", eng)

            @block.scalar
            def _(eng):
                emit_engine("scalar", eng)

            @block.gpsimd
            def _(eng):
                emit_engine("gpsimd", eng)


O_Q, O_KV, O_G, O_POOL, O_CONV, O_F, O_FF, O_GATE, O_MERGE = 0, 256, 640, 652, 908, 1676, 2444, 2448, 3472
NU_MAIN = 17


def _sw(cols):
    cols = np.asarray(cols)
    out = cols.copy()
    out[0:8] = cols[8:16]
    out[8:16] = cols[0:8]
    return out


def _unit_cols():
    ar = np.arange
    q = [O_Q + 64 * h + ar(64) for h in range(4)]
    k_c, v_c, k_s, v_s, k_w, v_w = [O_KV + 64 * i + ar(64) for i in range(6)]
    f_q = O_F + ar(256)
    f_k = O_F + 256 + ar(256)
    f_v = O_F + 512 + ar(256)
    cat = np.concatenate
    units = []
    units.append(cat([q[0], q[1], _sw(q[0]), _sw(q[1]), q[2], q[3], _sw(q[2]), _sw(q[3])]))
    units.append(cat([k_s, k_s, _sw(k_s), _sw(k_s), k_w, k_w, _sw(k_w), _sw(k_w)]))
    units.append(cat([k_c, v_c, f_q[0:128], f_q[128:256], f_k[0:128]]))
    units.append(cat([f_k[128:256], v_s, v_w, f_v]))
    units.append(cat([O_POOL + ar(256), O_CONV + ar(256)]))
    units.append(cat([O_CONV + 256 + ar(256), O_CONV + 512 + ar(256)]))
    units.append(cat([O_GATE + 256 + ar(256), O_GATE + 512 + ar(256)]))
    units.append(cat([O_GATE + ar(256), O_GATE + 768 + ar(256)]))
    units.append(cat([O_G + ar(12), O_FF + ar(4)]))
    for c in range(8):
        units.append(O_MERGE + c * 512 + ar(512))
    return units


def prep_layer(inp, l):
    f = np.float32
    w_in = np.asarray(inp["w_in"][l], f)
    wmain = np.zeros((NU_MAIN, 128, 8, 512), f)
    for u, cols in enumerate(_unit_cols()):
        w = w_in[:, cols]
        wmain[u, :, :, :w.shape[1]] = w.reshape(8, 128, -1).transpose(1, 0, 2)
    wb = np.asarray(inp["w_branch"][l], f)
    wbr = np.zeros((8, 128, 2, 512), f)
    for c in range(8):
        i, half = c // 2, c % 2
        wbr[c] = wb[i][:, half * 512:(half + 1) * 512].reshape(2, 128, 512).transpose(1, 0, 2)
    wo_ = np.asarray(inp["w_out"][l], f)
    wo = np.zeros((2, 128, 8, 512), f)
    for half in range(2):
        wo[half] = wo_[:, half * 512:(half + 1) * 512].reshape(8, 128, 512).transpose(1, 0, 2)
    w1 = np.asarray(inp["cmp_w1"][l], f)
    w1kv = np.zeros((128, 32, 128), f)
    poskv = np.zeros((128, 32), f)
    pos = np.asarray(inp["cmp_pos"][l], f)
    for kv in range(2):
        w1kv[64 * kv:64 * kv + 64] = w1[kv].reshape(32, 64, 128).transpose(1, 0, 2)
        poskv[64 * kv:64 * kv + 64] = pos[kv].T
    w2 = np.asarray(inp["cmp_w2"][l], f)
    b2 = np.asarray(inp["cmp_b2"][l], f)
    swi = _sw(np.arange(64))
    w2k = np.zeros((128, 2, 128), f)
    w2k[:, 0, :] = np.concatenate([w2[0], w2[0]], axis=1)
    w2k[:, 1, :] = np.concatenate([w2[0][:, swi], w2[0][:, swi]], axis=1)
    pw = np.asarray(inp["pool_w"][l], f)
    pool_bd = np.zeros((128, 2, 128), f)
    for c in range(2):
        for r in range(2):
            pool_bd[64 * r:64 * r + 64, c, 64 * r:64 * r + 64] = pw[2 * c + r]
    small = np.zeros((128, 24), f)
    small[:, 0:8] = np.asarray(inp["norm_g"][l], f).reshape(8, 128).T
    small[:, 8:10] = np.asarray(inp["cmp_b1"][l], f).T
    small[:, 10] = np.concatenate([b2[0], b2[0]])
    small[:, 11] = np.concatenate([b2[0][swi], b2[0][swi]])
    small[:, 12:14] = np.asarray(inp["pool_scale"][l], f).reshape(2, 128).T
    cw = np.asarray(inp["conv_w"][l], f)
    for c in range(2):
        small[:, 14 + 3 * c:17 + 3 * c] = cw[:, c * 128:(c + 1) * 128].T
    return dict(wmain=wmain, wbr=wbr, wo=wo, w1kv=w1kv, poskv=poskv, w2k=w2k,
                w2v=np.ascontiguousarray(w2[1]), b2v=np.ascontiguousarray(b2[1][None, :]),
                pool_bd=pool_bd, small=small)


def prep_consts(T, depth, inp):
    f = np.float32
    NT = T // 128
    NCP = T // 16
    c = {}
    c["ident"] = np.eye(128, dtype=f)
    p = np.arange(128)[:, None]
    fr = np.arange(128)[None, :]
    c["tri"] = (fr >= p).astype(f)
    c["ntri"] = (fr < p).astype(f)
    c["ones"] = np.ones((128, 128), f)
    kg = np.arange(T)[None, :]
    c["efull"] = (kg // 64 == np.arange(64)[:, None]).astype(f)
    half = 8
    inv = (500000.0 ** (-np.arange(half, dtype=np.float32) / half)).astype(np.float32)

    def rope_tab(posv):
        ang = posv.astype(np.float32)[:, None] * inv[None, :]
        cs = np.cos(ang).astype(f).T
        sn = np.sin(ang).astype(f).T
        ct = np.ones((64, posv.shape[0]), f)
        st = np.zeros((64, posv.shape[0]), f)
        ct[0:8] = cs
        ct[8:16] = cs
        st[0:8] = -sn
        st[8:16] = sn
        return np.concatenate([ct, ct], 0), np.concatenate([st, st], 0)

    c["ropec"], c["ropes"] = rope_tab(np.arange(T))
    c["cmpc"], c["cmps"] = rope_tab(np.arange(NCP) * 16 + 31)
    c["cmask"] = (16 * np.arange(128)[:, None] + 31 <= np.arange(T)[None, :]).astype(f)
    t = np.arange(T)[:, None]
    j = np.arange(T // 64)[None, :]
    jt = t // 64
    forced = (j == 0) | (j == jt) | (j == jt - 1)
    valid = j * 64 <= t
    addt = np.where(valid, np.where(forced, 1.0e4, 0.0), -1.0e30).astype(f)
    addtab = np.zeros((NT, 128, 64), f)
    addtab[:, :, :T // 64] = addt.reshape(NT, 128, T // 64)
    if T // 64 < 64:
        addtab[:, :, T // 64:] = -1.0e30
    c["addtab"] = addtab
    n = np.arange(NCP)[:, None]
    ovl = ((n * 16 < j * 64 + 64) & (n * 16 + 32 > j * 64)).astype(f)
    ovl[NCP - 1] = 0.0
    ov = np.zeros((NCP, 64), f)
    ov[:, :T // 64] = ovl
    nct = max(1, NCP // 128)
    c["ovl"] = np.ascontiguousarray(ov.reshape(nct, -1, 64).transpose(1, 0, 2)) if NCP >= 128 else ov[:, None, :]
    pw = np.array([2, 4, 8, 16], f)
    invw = np.zeros((128, 2), f)
    invc = np.zeros((128, 2, 16), f)
    for ch in range(2):
        for r in range(2):
            w = pw[2 * ch + r]
            invw[64 * r:64 * r + 64, ch] = 1.0 / w
            invc[64 * r:64 * r + 64, ch, :] = 1.0 / np.minimum(np.arange(1, 17), w)
    c["invw"] = invw
    c["invc"] = invc
    c["fgbc"] = np.ascontiguousarray(np.broadcast_to(np.asarray(inp["final_norm_g"], f)[None, :], (128, D)))
    c["fbbc"] = np.ascontiguousarray(np.broadcast_to(np.asarray(inp["fox_f_bias"], f)[:depth].reshape(1, -1), (128, depth * 4)))
    return c


LAYER_KEYS = ("wmain", "wbr", "wo", "w1kv", "poskv", "w2k", "w2v", "b2v", "pool_bd", "small")


def build_program(T, DEPTH, NSEQ, QI, const_shapes, layer_shapes):
    G = 128 * QI
    NG = T // G
    NT = T // 128
    NCP = T // 16
    NC = NCP - 1
    NCT = max(1, NCP // 128)
    NCW = min(128, NCP)
    NB = T // 64

    nc = bass.Bass("TRN2", target_bir_lowering=False)
    xin = nc.dram_tensor("x", [NSEQ * T, D], F32, kind="ExternalInput").ap()
    yout = nc.dram_tensor("y", [NSEQ * T, D], F32, kind="ExternalOutput").ap()
    dbg = nc.dram_tensor("dbg", [NSEQ * T, 512], F32, kind="ExternalOutput").ap() if DEBUG else None
    xmid = [nc.dram_tensor("xmid%d" % i, [NSEQ * T, D], F32).ap() for i in range(max(0, DEPTH - 1))]
    cd = {k: nc.dram_tensor("c_" + k, list(s), F32, kind="ExternalInput").ap() for k, s in const_shapes.items()}
    ld = [{k: nc.dram_tensor("l%d_%s" % (l, k), list(s), F32, kind="ExternalInput").ap()
           for k, s in layer_shapes.items()} for l in range(DEPTH)]

    BSH = dict(wmain=[NU_MAIN, 128, 8, 512], wbr=[8, 128, 2, 512], wo=[2, 128, 8, 512], w1kv=[128, 32, 128],
               poskv=[128, 32], w2k=[128, 2, 128], w2v=[128, 64], b2v=[1, 64], pool_bd=[128, 2, 128])
    wdb = [{k: nc.dram_tensor("b%d_%s" % (l, k), shp, BF16).ap() for k, shp in BSH.items()} for l in range(DEPTH)]

    st = contextlib.ExitStack()
    with st:
        def sb(name, shape, dt=F32):
            return st.enter_context(nc.sbuf_tensor(name, list(shape), dt))

        def ps(name, shape, dt=F32):
            return st.enter_context(nc.psum_tensor(name, list(shape), dt))

        S = Sched(nc)

        def V(fn, r, w):
            S.add("vector", fn, r, w)

        def A(fn, r, w):
            S.add("scalar", fn, r, w)

        def MM(out, lhsT, rhs, start, stop, r, w, skip=False):
            S.add("tensor", lambda e: e.matmul(out, lhsT=lhsT, rhs=rhs, start=start, stop=stop,
                                               skip_group_check=skip), r, w)

        psP = ps("psP", [128, 2, 512])
        psS = ps("psS", [128, 3, 512])
        psM = ps("psM", [128, 512])
        acc = ps("acc", [128, 2, 512])

        def slot(i, w):
            per = 512 // w
            return acc[:, i // per, (i % per) * w:(i % per) * w + w]

        def slot_keys(n, w):
            per = 512 // w
            return [("acc", b) for b in range((n + per - 1) // per)]

        ident = sb("ident", [128, 128], BF16)
        tri = sb("tri", [128, 128], BF16)
        ntri = sb("ntri", [128, 128], BF16)
        trif = sb("trif", [128, 128], F32)
        onesf = sb("onesf", [128, 128], F32)
        onesb = sb("onesb", [1, 128], BF16)
        efull = sb("efull", [64, T], BF16)
        cmpc = sb("cmpc", [128, NCP], F32)
        cmps = sb("cmps", [128, NCP], F32)
        invw = sb("invw", [128, 2], F32)
        invc = sb("invc", [128, 2, 16], F32)
        fgbc = sb("fgbc", [128, D], F32)
        fbbc = sb("fbbc", [128, DEPTH * 4], F32)
        S.dma(ident[:], cd["ident"], writes=["ident"], eng="gpsimd")
        S.dma(tri[:], cd["tri"], writes=["tri"], eng="gpsimd")
        S.dma(ntri[:], cd["ntri"], writes=["ntri"], eng="gpsimd")
        S.dma(onesb[:], cd["ones"][0:1, :], writes=["onesb"], eng="gpsimd")
        S.dma(efull[:], cd["efull"], writes=["efull"], eng="gpsimd")
        S.dma(trif[:], cd["tri"], writes=["trif"])
        S.dma(onesf[:], cd["ones"], writes=["onesf"])
        S.dma(cmpc[:], cd["cmpc"], writes=["cmpc"])
        S.dma(cmps[:], cd["cmps"], writes=["cmps"])
        S.dma(invw[:], cd["invw"], writes=["invw"])
        S.dma(invc[:], cd["invc"], writes=["invc"])
        S.dma(fgbc[:], cd["fgbc"], writes=["fgbc"])
        S.dma(fbbc[:], cd["fbbc"], writes=["fbbc"])

        for l in range(DEPTH):
            for k in ("w1kv", "poskv", "w2k", "w2v", "b2v", "pool_bd"):
                S.dma(wdb[l][k], ld[l][k], writes=[("wd", l, k, 0)], eng="gpsimd")
            for k, n in (("wmain", NU_MAIN), ("wbr", 8), ("wo", 2)):
                for u in range(n):
                    S.dma(wdb[l][k][u], ld[l][k][u], writes=[("wd", l, k, u)], eng="gpsimd")
        small = sb("small", [128, 24], F32)
        w1kv = sb("w1kv", [128, 32, 128], BF16)
        poskv = sb("poskv", [128, 32], BF16)
        w2k = sb("w2k", [128, 2, 128], BF16)
        w2v = sb("w2v", [128, 64], BF16)
        b2v = sb("b2v", [1, 64], BF16)
        pool_bd = sb("pool_bd", [128, 2, 128], BF16)
        c1 = sb("c1", [128, 2], F32)

        KS2 = sb("KS2", [128, T], BF16)
        KW2 = sb("KW2", [128, T], BF16)
        KCVC = sb("KCVC", [128, T], BF16)
        FK = [sb("FK%d" % i, [128, T], BF16) for i in range(2)]
        VS = sb("VS", [128, NT, 65], BF16)
        VW = sb("VW", [128, NT, 65], BF16)
        FV = sb("FV", [128, NT, 4, 65], BF16)
        KCC = sb("KCC", [128, NCT * 128], BF16)
        VCA = sb("VCA", [128, NCT, 129], BF16)
        CKR = sb("CKR", [128, NT, 8], F32)
        RSUM = sb("RSUM", [128, 4], F32)

        NWB = 3
        wb = [sb("wb%d" % i, [128, 10, 512], BF16) for i in range(NWB)]
        wcnt = [0]
        wissued = [0]
        NG_ = T // (128 * QI)
        unit_seq = []
        for l_ in range(DEPTH):
            per = []
            for u in range(9):
                ncol = 16 if u == 8 else 512
                per.append([(0, 8, ncol, wdb[l_]["wmain"][u][:, :, 0:ncol], ("wd", l_, "wmain", u))])
            for c in range(8):
                per.append([(0, 8, 512, wdb[l_]["wmain"][9 + c], ("wd", l_, "wmain", 9 + c)),
                            (8, 10, 512, wdb[l_]["wbr"][c], ("wd", l_, "wbr", c))])
            for hf in range(2):
                per.append([(0, 8, 512, wdb[l_]["wo"][hf], ("wd", l_, "wo", hf))])
            for _ in range(NSEQ * NG_):
                unit_seq.extend(per)

        def next_unit():
            k = wcnt[0]
            wcnt[0] += 1
            while wissued[0] < min(len(unit_seq), k + NWB) and wissued[0] <= k + 2:
                j = wissued[0]
                wissued[0] += 1
                if not S.enabled:
                    continue
                for lo, hi, ncol, ap, dkey in unit_seq[j]:
                    S.dma(wb[j % NWB][:, lo:hi, 0:ncol], ap, reads=[dkey], writes=[("wb", j % NWB)])
            return wb[k % NWB], ("wb", k % NWB)

        xg = sb("xg", [128, QI, D], F32)
        sq = sb("sq", [128, D], F32)
        ss = sb("ss", [128, QI], F32)
        rstd = sb("rstd", [128, QI], F32)
        xn = sb("xn", [128, D], BF16)
        hT = sb("hT", [128, 8, G], BF16)
        QN = [sb("QN%d" % i, [128, G], BF16) for i in range(2)]
        FQ = [sb("FQ%d" % i, [128, G], BF16) for i in range(2)]
        ropeC = sb("ropeC", [128, G], F32)
        ropeS = sb("ropeS", [128, G], F32)
        rt1 = sb("rt1", [128, G], F32)
        rt2 = sb("rt2", [128, G], F32)
        GPC = sb("GPC", [128, 4, G], BF16)
        GNF = sb("GNF", [128, QI, 512], BF16)
        NGt = sb("NGt", [128, QI, 12], F32)
        LF = sb("LF", [128, QI, 4], F32)
        LFt = sb("LFt", [128, QI, 4], F32)
        PIN = sb("PIN", [128, 2, 16 + G], F32)
        CX = sb("CX", [128, 2, G], F32)
        CB = sb("CB", [128, 2, G], F32)
        CU = sb("CU", [128, 2, 2 + G], F32)
        PA = sb("PA", [128, 2, 16 + G], F32)
        PB = sb("PB", [128, 2, 16 + G], F32)
        PL = sb("PL", [128, 2, G], BF16)
        PT16 = sb("PT16", [128, 16], F32)
        CY = sb("CY", [128, G], F32)
        ogT = sb("ogT", [128, 8, G], BF16)
        ACCM = sb("ACCM", [128, QI, D], F32)
        ACCB = sb("ACCB", [128, D], BF16)
        ONSA = sb("ONSA", [128, QI, 256], F32)
        OFOX = sb("OFOX", [128, QI, 256], F32)
        OG = sb("OG", [128, 256], BF16)
        IMP = sb("IMP", [128, QI, 64], F32)
        IMPM = sb("IMPM", [128, 64], F32)
        IMPM2 = sb("IMPM2", [128, 64], F32)
        addt = sb("addt", [128, 64], F32)
        m8a = sb("m8a", [128, 8], F32)
        m8b = sb("m8b", [128, 8], F32)
        sel = sb("sel", [128, 64], BF16)
        selT = sb("selT", [64, G], BF16)
        ET = [sb("ET%d" % i, [128, G], BF16) for i in range(3)]
        PTt = [sb("PTt%d" % i, [128, G], BF16) for i in range(3)]
        PTC = [sb("PTC%d" % i, [128, G], BF16) for i in range(NCT)]
        MT = [sb("MT%d" % i, [128, G], BF16) for i in range(2)]
        cm = [sb("cm%d" % i, [128, G], F32) for i in range(NCT)]
        BT = sb("BT", [128, 4, QI, NT], F32)
        SM = sb("SM", [128, 512], F32)
        TMPt = sb("TMPt", [128, 512], F32)
        den = sb("den", [128, 1], F32)
        rden = sb("rden", [128, 1], F32)
        scl = sb("scl", [128, 1], F32)
        hid = [sb("hid%d" % i, [128, 32], BF16) for i in range(2)]
        kt1 = sb("kt1", [128, 32], F32)
        kt2 = sb("kt2", [128, 32], F32)
        vstage = sb("vstage", [32, 64], BF16)
        YO = sb("YO", [128, D], F32)
        fence = sb("fence", [128, 1], F32)
        den8 = sb("den8", [128, 16], F32)
        rden8 = sb("rden8", [128, 16], F32)
        scl8 = sb("scl8", [128, 16], F32)
        U8s = sb("U8s", [128, 64], F32)
        epsb = sb("epsb", [128, 1], F32)
        oneb = sb("oneb", [128, 1], F32)
        V(lambda e: e.memset(epsb[:], 1e-6), [], ["epsb"])
        V(lambda e: e.memset(oneb[:], 1.0), [], ["oneb"])
        cnt = {"e": 0, "p": 0, "m": 0, "s": 0, "t": 0}

        V(lambda e: e.memset(VS[:, :, 64:65], 1.0), [], ["VS1"])
        V(lambda e: e.memset(VW[:, :, 64:65], 1.0), [], ["VW1"])
        V(lambda e: e.memset(FV[:, :, :, 64:65], 1.0), [], ["FV1"])
        V(lambda e: e.memset(VCA[:, :, 64:65], 1.0), [], ["VCA1"])
        V(lambda e: e.memset(VCA[:, :, 65:129], 0.0), [], ["VCAo"])
        S.dma(VCA[0:NCW, :, 65:129], cd["ovl"], writes=["VCAo"], eng="gpsimd")

        for l in range(DEPTH):
            L = ld[l]
            lastl = (l == DEPTH - 1)
            xsrc = xin if l == 0 else xmid[l - 1]
            xdst = yout if lastl else xmid[l]
            S.dma(small[:], L["small"], writes=["small"])
            S.dma(w1kv[:], wdb[l]["w1kv"], reads=[("wd", l, "w1kv", 0)], writes=["w1kv"])
            S.dma(poskv[:], wdb[l]["poskv"], reads=[("wd", l, "poskv", 0)], writes=["poskv"])
            S.dma(w2k[:], wdb[l]["w2k"], reads=[("wd", l, "w2k", 0)], writes=["w2k"])
            S.dma(w2v[:], wdb[l]["w2v"], reads=[("wd", l, "w2v", 0)], writes=["w2v"])
            S.dma(b2v[:], wdb[l]["b2v"], reads=[("wd", l, "b2v", 0)], writes=["b2v"])
            S.dma(pool_bd[:], wdb[l]["pool_bd"], reads=[("wd", l, "pool_bd", 0)], writes=["pool_bd"])
            for kv in range(2):
                b = 64 * kv
                for li in range(32):
                    MM(psM[:, kv:kv + 1], w1kv[b:b + 64, li, :], poskv[b:b + 64, li:li + 1],
                       li == 0, li == 31, ["w1kv", "poskv"], ["psM"])
                V(lambda e, kv=kv: e.tensor_tensor(out=c1[:, kv:kv + 1], in0=psM[:, kv:kv + 1],
                                                   in1=small[:, 8 + kv:9 + kv], op=ALU.add),
                  ["psM", "small"], ["c1"])

            for s in range(NSEQ):
                r0 = s * T
                V(lambda e: e.memset(KCC[:], 0.0), [], ["KCC"])
                V(lambda e: e.memset(VCA[:, :, 0:64], 0.0), [], ["VCA"])
                V(lambda e: e.memset(RSUM[:], 0.0), [], ["RSUM"])
                V(lambda e: e.memset(PIN[:, :, 0:16], 0.0), [], ["PINh"])
                V(lambda e: e.memset(CU[:, :, 0:2], 0.0), [], ["CUh"])

                for g in range(NG):
                    t0 = g * G
                    tile0 = g * QI
                    S.dma(ropeC[:], cd["ropec"][:, t0:t0 + G], writes=["ropeC"])
                    S.dma(ropeS[:], cd["ropes"][:, t0:t0 + G], writes=["ropeS"])
                    for qi in range(QI):
                        rr = r0 + t0 + qi * 128
                        S.dma(xg[:, qi, :], xsrc[rr:rr + 128, :], reads=[("xd", l, s, g, qi)], writes=[("xg", qi)])
                    for qi in range(QI):
                        A(lambda e, qi=qi: e.activation(out=sq[:], in_=xg[:, qi, :], func=AF.Square),
                          [("xg", qi)], ["sq"])
                        V(lambda e, qi=qi: e.tensor_reduce(out=ss[:, qi:qi + 1], in_=sq[:], axis=mybir.AxisListType.X, op=ALU.add),
                          ["sq"], [("ss", qi)])
                        A(lambda e, qi=qi: e.activation(out=rstd[:, qi:qi + 1], in_=ss[:, qi:qi + 1], func=AF.Sqrt,
                                                        scale=1.0 / D, bias=epsb[:, 0:1]),
                          [("ss", qi), "epsb"], [("rstd", qi)])
                        V(lambda e, qi=qi: e.reciprocal(out=rstd[:, qi:qi + 1], in_=rstd[:, qi:qi + 1]),
                          [("rstd", qi)], [("rstd", qi)])
                        V(lambda e, qi=qi: e.tensor_scalar(out=xn[:], in0=xg[:, qi, :], scalar1=rstd[:, qi:qi + 1],
                                                           scalar2=None, op0=ALU.mult),
                          [("xg", qi), ("rstd", qi)], ["xn"])
                        for c in range(8):
                            MM(psP[:, c // 4, (c % 4) * 128:(c % 4) * 128 + 128], xn[:, c * 128:(c + 1) * 128], ident[:],
                               True, True, ["xn", "ident"], [("psP", c // 4)])
                        for c in range(8):
                            V(lambda e, c=c, qi=qi: e.tensor_scalar(
                                out=hT[:, c, qi * 128:(qi + 1) * 128],
                                in0=psP[:, c // 4, (c % 4) * 128:(c % 4) * 128 + 128],
                                scalar1=small[:, c:c + 1], scalar2=None, op0=ALU.mult),
                              [("psP", c // 4), "small"], [("hT", qi)])
                    hT_keys = [("hT", qi) for qi in range(QI)]

                    def fm_block(wt, wkey, blk, bank):
                        for kc in range(8):
                            MM(psP[:, bank, 0:G], wt[:, kc, blk * 128:(blk + 1) * 128], hT[:, kc, :],
                               kc == 0, kc == 7, hT_keys + [wkey], [("psP", bank)])

                    def unit_main(u, ncol=512):
                        return next_unit()

                    def rope_pair(wt, wkey, blk, dst_ap, dst_key):
                        fm_block(wt, wkey, blk, 0)
                        fm_block(wt, wkey, blk + 1, 1)
                        V(lambda e: e.tensor_tensor(out=rt1[:], in0=psP[:, 0, 0:G], in1=ropeC[:], op=ALU.mult),
                          [("psP", 0), "ropeC"], ["rt1"])
                        V(lambda e: e.tensor_tensor(out=rt2[:], in0=psP[:, 1, 0:G], in1=ropeS[:], op=ALU.mult),
                          [("psP", 1), "ropeS"], ["rt2"])
                        V(lambda e: e.tensor_tensor(out=dst_ap, in0=rt1[:], in1=rt2[:], op=ALU.add),
                          ["rt1", "rt2"], [dst_key])

                    def fm_copy(wt, wkey, blk, dst_ap, dst_key, func=AF.Copy):
                        bank = cnt["p"] % 2
                        cnt["p"] += 1
                        fm_block(wt, wkey, blk, bank)
                        A(lambda e: e.activation(out=dst_ap, in_=psP[:, bank, 0:G], func=func),
                          [("psP", bank)], [dst_key])

                    S.enabled = STAGE >= 2
                    wt, wk = unit_main(0)
                    rope_pair(wt, wk, 0, QN[0][:], "QN0")
                    rope_pair(wt, wk, 2, QN[1][:], "QN1")
                    S.enabled = STAGE >= 2.1
                    wt, wk = unit_main(1)
                    rope_pair(wt, wk, 0, KS2[:, t0:t0 + G], ("KS2", g))
                    rope_pair(wt, wk, 2, KW2[:, t0:t0 + G], ("KW2", g))
                    S.enabled = STAGE >= 2.2
                    wt, wk = unit_main(2)
                    fm_copy(wt, wk, 0, KCVC[:, t0:t0 + G], ("KCVC", g))
                    fm_copy(wt, wk, 1, FQ[0][:], "FQ0")
                    fm_copy(wt, wk, 2, FQ[1][:], "FQ1")
                    fm_copy(wt, wk, 3, FK[0][:, t0:t0 + G], ("FK0", g))
                    S.enabled = STAGE >= 2.3
                    wt, wk = unit_main(3)
                    fm_copy(wt, wk, 0, FK[1][:, t0:t0 + G], ("FK1", g))
                    for qi in range(QI):
                        bank = cnt["p"] % 2
                        cnt["p"] += 1
                        ti = tile0 + qi
                        for kc in range(8):
                            MM(psP[:, bank, 0:384], hT[:, kc, qi * 128:(qi + 1) * 128], wt[:, kc, 128:512],
                               kc == 0, kc == 7, [("hT", qi), wk], [("psP", bank)])
                        A(lambda e, bank=bank, ti=ti: e.activation(out=VS[:, ti, 0:64], in_=psP[:, bank, 0:64], func=AF.Copy),
                          [("psP", bank)], [("VS", ti)])
                        A(lambda e, bank=bank, ti=ti: e.activation(out=VW[:, ti, 0:64], in_=psP[:, bank, 64:128], func=AF.Copy),
                          [("psP", bank)], [("VW", ti)])
                        for hh in range(4):
                            A(lambda e, bank=bank, ti=ti, hh=hh: e.activation(
                                out=FV[:, ti, hh, 0:64], in_=psP[:, bank, 128 + 64 * hh:192 + 64 * hh], func=AF.Copy),
                              [("psP", bank)], [("FV", ti)])
                    S.enabled = STAGE >= 2.4
                    wt, wk = unit_main(4)
                    fm_copy(wt, wk, 0, PIN[:, 0, 16:16 + G], ("PIN", 0))
                    fm_copy(wt, wk, 1, PIN[:, 1, 16:16 + G], ("PIN", 1))
                    fm_copy(wt, wk, 2, CX[:, 0, :], ("CX", 0))
                    fm_copy(wt, wk, 3, CX[:, 1, :], ("CX", 1))
                    S.enabled = STAGE >= 2.5
                    wt, wk = unit_main(5)
                    fm_copy(wt, wk, 0, CB[:, 0, :], ("CB", 0))
                    fm_copy(wt, wk, 1, CB[:, 1, :], ("CB", 1))
                    for c in range(2):
                        bank = cnt["p"] % 2
                        cnt["p"] += 1
                        fm_block(wt, wk, 2 + c, bank)
                        V(lambda e, c=c, bank=bank: e.tensor_tensor(out=CU[:, c, 2:2 + G], in0=psP[:, bank, 0:G],
                                                                    in1=CX[:, c, :], op=ALU.mult),
                          [("psP", bank), ("CX", c)], [("CU", c)])
                    S.enabled = STAGE >= 2.6
                    wt, wk = unit_main(6)
                    for b in range(4):
                        fm_copy(wt, wk, b, GPC[:, b, :], ("GPC", b), func=AF.Silu)
                    S.enabled = STAGE >= 2.7
                    wt, wk = unit_main(7)
                    for qi in range(QI):
                        bank = cnt["p"] % 2
                        cnt["p"] += 1
                        for kc in range(8):
                            MM(psP[:, bank, :], hT[:, kc, qi * 128:(qi + 1) * 128], wt[:, kc, :],
                               kc == 0, kc == 7, [("hT", qi), wk], [("psP", bank)])
                        A(lambda e, bank=bank, qi=qi: e.activation(out=GNF[:, qi, :], in_=psP[:, bank, :], func=AF.Silu),
                          [("psP", bank)], [("GNF", qi)])
                    S.enabled = STAGE >= 2.8
                    wt, wk = unit_main(8, ncol=16)
                    for qi in range(QI):
                        for kc in range(8):
                            MM(psM[:, qi * 16:qi * 16 + 16], hT[:, kc, qi * 128:(qi + 1) * 128], wt[:, kc, 0:16],
                               kc == 0, kc == 7, [("hT", qi), wk], ["psM"])
                    A(lambda e: e.activation(out=U8s[:, 0:QI * 16], in_=psM[:, 0:QI * 16], func=AF.Copy), ["psM"], ["U8s"])
                    for qi in range(QI):
                        A(lambda e, qi=qi: e.activation(out=NGt[:, qi, :], in_=U8s[:, qi * 16:qi * 16 + 12], func=AF.Sigmoid),
                          ["U8s"], ["NGt"])
                    for qi in range(QI):
                        V(lambda e, qi=qi: e.tensor_tensor(out=LFt[:, qi, :], in0=U8s[:, qi * 16 + 12:qi * 16 + 16],
                                                           in1=fbbc[:, l * 4:l * 4 + 4], op=ALU.add),
                          ["U8s", "fbbc"], ["LFt"])
                    A(lambda e: e.activation(out=LFt[:], in_=LFt[:], func=AF.Exp, scale=-1.0), ["LFt"], ["LFt"])
                    A(lambda e: e.activation(out=LFt[:], in_=LFt[:], func=AF.Ln, bias=oneb[:, 0:1]), ["LFt", "oneb"], ["LFt"])
                    V(lambda e: e.tensor_scalar(out=LF[:], in0=LFt[:], scalar1=-1.0, scalar2=None, op0=ALU.mult),
                      ["LFt"], ["LF"])
                    S.enabled = STAGE >= 3
                    for qi in range(QI):
                        ti = tile0 + qi
                        MM(psM[:, 0:4], trif[:], LF[:, qi, :], True, False, ["trif", "LF"], ["psM"])
                        MM(psM[:, 0:4], onesf[:], RSUM[:], False, True, ["onesf", "RSUM"], ["psM"])
                        V(lambda e, qi=qi: e.tensor_tensor(out=RSUM[:], in0=RSUM[:], in1=LF[:, qi, :], op=ALU.add),
                          ["RSUM", "LF"], ["RSUM"])
                        MM(psM[:, 4:8], onesf[:], RSUM[:], True, True, ["onesf", "RSUM"], ["psM"])
                        V(lambda e, ti=ti: e.tensor_copy(out=CKR[:, ti, :], in_=psM[:, 0:8]), ["psM"], [("CKR", ti)])
                    S.enabled = STAGE >= 4
                    n0 = max(0, (t0 // 16) - 1)
                    n1 = min(NC - 1, (t0 + G) // 16 - 2)
                    nn = n1 - n0 + 1
                    kc_keys = [("KCVC", gg) for gg in range(g + 1)]
                    for kv in range(2):
                        b = 64 * kv
                        for li in range(32):
                            a0 = 16 * n0 + li
                            MM(psM[:, 0:nn], w1kv[b:b + 64, li, :], KCVC[b:b + 64, a0:a0 + 16 * (nn - 1) + 1:16],
                               li == 0, li == 31, ["w1kv"] + kc_keys, ["psM"])
                        A(lambda e, kv=kv: e.activation(out=hid[kv][:, 0:nn], in_=psM[:, 0:nn], func=AF.Silu,
                                                        bias=c1[:, kv:kv + 1]),
                          ["psM", "c1"], [("hid", kv)])
                    MM(psM[:, 0:nn], w2k[:, 0, :], hid[0][:, 0:nn], True, True, ["w2k", ("hid", 0)], ["psM"])
                    MM(psM[:, 32:32 + nn], w2k[:, 1, :], hid[0][:, 0:nn], True, True, ["w2k", ("hid", 0)], ["psM"])
                    V(lambda e: e.scalar_tensor_tensor(out=kt1[:, 0:nn], in0=psM[:, 0:nn], scalar=small[:, 10:11],
                                                       in1=cmpc[:, n0:n0 + nn], op0=ALU.add, op1=ALU.mult),
                      ["psM", "small", "cmpc"], ["kt1"])
                    V(lambda e: e.scalar_tensor_tensor(out=kt2[:, 0:nn], in0=psM[:, 32:32 + nn], scalar=small[:, 11:12],
                                                       in1=cmps[:, n0:n0 + nn], op0=ALU.add, op1=ALU.mult),
                      ["psM", "small", "cmps"], ["kt2"])
                    V(lambda e: e.tensor_tensor(out=KCC[:, n0:n0 + nn], in0=kt1[:, 0:nn], in1=kt2[:, 0:nn], op=ALU.add),
                      ["kt1", "kt2"], ["KCC"])
                    MM(psM[0:nn, 64:128], hid[1][:, 0:nn], w2v[:], True, False, [("hid", 1), "w2v"], ["psM"])
                    MM(psM[0:nn, 64:128], onesb[0:1, 0:nn], b2v[0:1, :], False, True, ["onesb", "b2v"], ["psM"])
                    V(lambda e: e.tensor_copy(out=vstage[0:nn, :], in_=psM[0:nn, 64:128]), ["psM"], ["vstage"])
                    i0 = 0
                    while i0 < nn:
                        na = n0 + i0
                        j, p0 = na // 128, na % 128
                        cntr = min(nn - i0, 128 - p0)
                        S.dma(VCA[p0:p0 + cntr, j, 0:64], vstage[i0:i0 + cntr, :], reads=["vstage"], writes=["VCA"])
                        i0 += cntr
                    S.enabled = STAGE >= 5
                    jvalid = [j for j in range(NCT) if 16 * (128 * j) + 31 <= t0 + G - 1]
                    for j in jvalid:
                        S.dma(cm[j][:], cd["cmask"][:, t0 - 2048 * j:t0 - 2048 * j + G], writes=[("cm", j)])
                    W = 129
                    for h in range(4):
                        pr, pb = h // 2, 64 * (h % 2)
                        akeys = slot_keys(QI, W)
                        for ak in akeys:
                            bk = ak[1]
                            V(lambda e, bk=bk: e.memset(acc[:, bk, :], 0.0), [], [ak])
                        for j in jvalid:
                            bS = cnt["s"] % 3
                            cnt["s"] += 1
                            MM(psS[:, bS, 0:G], KCC[pb:pb + 64, j * 128:(j + 1) * 128], QN[pr][pb:pb + 64, :],
                               True, True, ["KCC", "QN%d" % pr], [("psS", bS)])
                            be = cnt["e"] % 3
                            cnt["e"] += 1
                            A(lambda e, bS=bS, be=be: e.activation(out=ET[be][:], in_=psS[:, bS, 0:G], func=AF.Exp, scale=0.125),
                              [("psS", bS)], [("ET", be)])
                            V(lambda e, be=be, j=j: e.tensor_tensor(out=PTC[j][:], in0=ET[be][:], in1=cm[j][:], op=ALU.mult),
                              [("ET", be), ("cm", j)], [("PTC", j)])
                        for qi in range(QI):
                            for j in jvalid:
                                MM(slot(qi, W), PTC[j][:, qi * 128:(qi + 1) * 128], VCA[:, j, :], False, False,
                                   [("PTC", j), "VCA", "VCA1", "VCAo"], akeys, skip=True)
                        for qi in range(QI):
                            sl = slot(qi, W)
                            V(lambda e, sl=sl: e.tensor_scalar(out=den[:], in0=sl[:, 64:65], scalar1=1e-30, scalar2=None, op0=ALU.max),
                              akeys, ["den"])
                            V(lambda e: e.reciprocal(out=rden[:], in_=den[:]), ["den"], ["rden"])
                            if h == 0:
                                V(lambda e, sl=sl, qi=qi: e.tensor_scalar(out=IMP[:, qi, :], in0=sl[:, 65:129], scalar1=rden[:, 0:1],
                                                                          scalar2=None, op0=ALU.mult),
                                  akeys + ["rden"], [("IMP", qi)])
                            else:
                                V(lambda e, sl=sl, qi=qi: e.scalar_tensor_tensor(out=IMP[:, qi, :], in0=sl[:, 65:129], scalar=rden[:, 0:1],
                                                                                 in1=IMP[:, qi, :], op0=ALU.mult, op1=ALU.add),
                                  akeys + ["rden", ("IMP", qi)], [("IMP", qi)])
                            V(lambda e, qi=qi, h=h: e.tensor_tensor(out=scl[:], in0=rden[:], in1=NGt[:, qi, h:h + 1], op=ALU.mult),
                              ["rden", "NGt"], ["scl"])
                            V(lambda e, sl=sl, qi=qi, h=h: e.tensor_scalar(out=ONSA[:, qi, h * 64:(h + 1) * 64], in0=sl[:, 0:64],
                                                                           scalar1=scl[:, 0:1], scalar2=None, op0=ALU.mult),
                              akeys + ["scl"], [("ONSA", qi)])
                    for qi in range(QI):
                        S.dma(addt[:], cd["addtab"][tile0 + qi], writes=["addt"])
                        V(lambda e, qi=qi: e.tensor_tensor(out=IMPM[:], in0=IMP[:, qi, :], in1=addt[:], op=ALU.add),
                          [("IMP", qi), "addt"], ["IMPM"])
                        V(lambda e: e.max(out=m8a[:], in_=IMPM[:]), ["IMPM"], ["m8a"])
                        V(lambda e: e.match_replace(out=IMPM2[:], in_to_replace=m8a[:], in_values=IMPM[:], imm_value=-3.0e38),
                          ["IMPM", "m8a"], ["IMPM2"])
                        V(lambda e: e.memset(fence[:], 0.0), ["IMPM2"], ["IMPM2", "fence"])
                        V(lambda e: e.max(out=m8b[:], in_=IMPM2[:]), ["IMPM2"], ["m8b"])
                        V(lambda e: e.tensor_scalar(out=sel[:], in0=IMPM[:], scalar1=m8b[:, 7:8], scalar2=None, op0=ALU.is_ge),
                          ["IMPM", "m8b"], ["sel"])
                        MM(psM[0:64, qi * 128:(qi + 1) * 128], sel[:], ident[:], True, True, ["sel", "ident"], ["psM"])
                    A(lambda e: e.activation(out=selT[:], in_=psM[0:64, 0:G], func=AF.Copy), ["psM"], ["selT"])

                    S.enabled = STAGE >= 6
                    W = 65

                    def acc_reset():
                        ks = slot_keys(4 * QI, W)
                        for ak in ks:
                            bk = ak[1]
                            V(lambda e, bk=bk: e.memset(acc[:, bk, :], 0.0), [], [ak])
                        return ks

                    def evac(akeys, dst, gate_off, first):
                        ns = 4 * QI
                        per = 512 // W
                        for b0 in range(0, ns, per):
                            n = min(per, ns - b0)
                            V(lambda e, b0=b0, n=n: e.tensor_scalar(out=den8[:, b0:b0 + n], in0=acc[:, b0 // per, 64:64 + W * (n - 1) + 1:W],
                                                                    scalar1=1e-30, scalar2=None, op0=ALU.max),
                              akeys, ["den8"])
                        V(lambda e: e.reciprocal(out=rden8[:, 0:ns], in_=den8[:, 0:ns]), ["den8"], ["rden8"])
                        if gate_off is not None:
                            for qi in range(QI):
                                V(lambda e, qi=qi: e.tensor_tensor(out=scl8[:, qi:ns:QI], in0=rden8[:, qi:ns:QI],
                                                                   in1=NGt[:, qi, gate_off:gate_off + 4], op=ALU.mult),
                                  ["rden8", "NGt"], ["scl8"])
                        for h in range(4):
                            for qi in range(QI):
                                i = h * QI + qi
                                sl = slot(i, W)
                                if gate_off is None:
                                    V(lambda e, sl=sl, qi=qi, h=h, i=i: e.tensor_scalar(out=dst[:, qi, h * 64:(h + 1) * 64], in0=sl[:, 0:64],
                                                                                        scalar1=rden8[:, i:i + 1], scalar2=None, op0=ALU.mult),
                                      akeys + ["rden8"], [("OFOX", qi)])
                                else:
                                    V(lambda e, sl=sl, qi=qi, h=h, i=i: e.scalar_tensor_tensor(
                                        out=dst[:, qi, h * 64:(h + 1) * 64], in0=sl[:, 0:64], scalar=scl8[:, i:i + 1],
                                        in1=dst[:, qi, h * 64:(h + 1) * 64], op0=ALU.mult, op1=ALU.add),
                                      akeys + ["scl8", ("ONSA", qi)], [("ONSA", qi)])

                    pend = []

                    def flush(keep=0):
                        while len(pend) > keep:
                            for f_ in pend.pop(0):
                                f_()

                    def defer_mm(o_, l_, r_, rd_, ak_):
                        pend[-1].append(lambda: MM(o_, l_, r_, False, False, rd_, ak_, skip=True))

                    akeys = acc_reset()
                    nkt = tile0 + QI
                    for kt in range(nkt):
                        di = kt - tile0
                        bm = cnt["m"] % 2
                        cnt["m"] += 1
                        MM(psM[:, 0:G], efull[:, kt * 128:(kt + 1) * 128], selT[:], True, True, ["efull", "selT"], ["psM"])
                        A(lambda e, bm=bm: e.activation(out=MT[bm][:], in_=psM[:, 0:G], func=AF.Copy), ["psM"], [("MT", bm)])
                        if di >= 0:
                            V(lambda e, bm=bm, di=di: e.tensor_tensor(out=MT[bm][:, di * 128:(di + 1) * 128],
                                                                      in0=MT[bm][:, di * 128:(di + 1) * 128], in1=tri[:], op=ALU.mult),
                              [("MT", bm), "tri"], [("MT", bm)])
                        for h in range(4):
                            pr, pb = h // 2, 64 * (h % 2)
                            bS = cnt["s"] % 3
                            cnt["s"] += 1
                            MM(psS[:, bS, 0:G], KS2[pb:pb + 64, kt * 128:(kt + 1) * 128], QN[pr][pb:pb + 64, :],
                               True, True, [("KS2", kt // QI), "QN%d" % pr], [("psS", bS)])
                            flush(1)
                            pend.append([])
                            be = cnt["e"] % 3
                            cnt["e"] += 1
                            A(lambda e, bS=bS, be=be: e.activation(out=ET[be][:], in_=psS[:, bS, 0:G], func=AF.Exp, scale=0.125),
                              [("psS", bS)], [("ET", be)])
                            bp = cnt["t"] % 3
                            cnt["t"] += 1
                            V(lambda e, be=be, bm=bm, bp=bp: e.tensor_tensor(out=PTt[bp][:], in0=ET[be][:], in1=MT[bm][:], op=ALU.mult),
                              [("ET", be), ("MT", bm)], [("PTt", bp)])
                            for qi in range(max(0, di), QI):
                                defer_mm(slot(h * QI + qi, W), PTt[bp][:, qi * 128:(qi + 1) * 128], VS[:, kt, :],
                                         [("PTt", bp), ("VS", kt), "VS1"], akeys)
                    flush()
                    evac(akeys, ONSA, 4, False)

                    S.enabled = STAGE >= 7
                    akeys = acc_reset()
                    for kt in range(max(0, tile0 - 4), nkt):
                        rel = [(qi, tile0 + qi - kt) for qi in range(QI)]
                        qis = [qi for qi, r in rel if 0 <= r <= 4]
                        if not qis:
                            continue
                        for h in range(4):
                            pr, pb = h // 2, 64 * (h % 2)
                            bS = cnt["s"] % 3
                            cnt["s"] += 1
                            MM(psS[:, bS, 0:G], KW2[pb:pb + 64, kt * 128:(kt + 1) * 128], QN[pr][pb:pb + 64, :],
                               True, True, [("KW2", kt // QI), "QN%d" % pr], [("psS", bS)])
                            flush(1)
                            pend.append([])
                            be = cnt["e"] % 3
                            cnt["e"] += 1
                            A(lambda e, bS=bS, be=be: e.activation(out=ET[be][:], in_=psS[:, bS, 0:G], func=AF.Exp, scale=0.125),
                              [("psS", bS)], [("ET", be)])
                            for qi, r in rel:
                                if r == 0 or r == 4:
                                    mk = tri if r == 0 else ntri
                                    V(lambda e, be=be, qi=qi, mk=mk: e.tensor_tensor(
                                        out=ET[be][:, qi * 128:(qi + 1) * 128], in0=ET[be][:, qi * 128:(qi + 1) * 128],
                                        in1=mk[:], op=ALU.mult),
                                      [("ET", be), "tri", "ntri"], [("ET", be)])
                            for qi in qis:
                                defer_mm(slot(h * QI + qi, W), ET[be][:, qi * 128:(qi + 1) * 128], VW[:, kt, :],
                                         [("ET", be), ("VW", kt), "VW1"], akeys)
                    flush()
                    evac(akeys, ONSA, 8, False)

                    S.enabled = STAGE >= 8
                    akeys = acc_reset()
                    for h in range(4):
                        for qi in range(QI):
                            nk = tile0 + qi + 1
                            V(lambda e, h=h, qi=qi, nk=nk: e.tensor_scalar(
                                out=BT[:, h, qi, 0:nk], in0=CKR[:, 0:nk, h], scalar1=-1.0,
                                scalar2=CKR[:, tile0 + qi, 4 + h:5 + h], op0=ALU.mult, op1=ALU.add),
                              [("CKR", ti) for ti in range(nk)], ["BT"])
                    GE = 16 + G
                    pin_keys = [("PIN", 0), ("PIN", 1), "PINh"]
                    V(lambda e: e.tensor_tensor(out=PA[:, :, 1:GE], in0=PIN[:, :, 1:GE], in1=PIN[:, :, 0:GE - 1], op=ALU.add),
                      pin_keys, ["PA"])

                    def pool_out(src, ch, plo):
                        V(lambda e: e.scalar_tensor_tensor(out=PL[plo:plo + 64, ch, :], in0=src[plo:plo + 64, ch, 16:GE],
                                                           scalar=invw[plo:plo + 64, ch:ch + 1], in1=PIN[plo:plo + 64, ch, 16:GE],
                                                           op0=ALU.mult, op1=ALU.subtract),
                          ["PA", "PB", "invw"] + pin_keys, [("PL", ch, plo)])
                        if g == 0:
                            V(lambda e: e.tensor_tensor(out=PT16[plo:plo + 64, :], in0=src[plo:plo + 64, ch, 16:32],
                                                        in1=invc[plo:plo + 64, ch, :], op=ALU.mult),
                              ["PA", "PB", "invc"], ["PT16"])
                            V(lambda e: e.tensor_tensor(out=PL[plo:plo + 64, ch, 0:16], in0=PT16[plo:plo + 64, :],
                                                        in1=PIN[plo:plo + 64, ch, 16:32], op=ALU.subtract),
                              ["PT16"] + pin_keys, [("PL", ch, plo)])

                    pool_out(PA, 0, 0)
                    V(lambda e: e.tensor_tensor(out=PB[:, :, 3:GE], in0=PA[:, :, 3:GE], in1=PA[:, :, 1:GE - 2], op=ALU.add),
                      ["PA"], ["PB"])
                    pool_out(PB, 0, 64)
                    V(lambda e: e.tensor_tensor(out=PA[:, 1, 7:GE], in0=PB[:, 1, 7:GE], in1=PB[:, 1, 3:GE - 4], op=ALU.add),
                      ["PB", ("PL", 0, 0)], ["PA"])
                    pool_out(PA, 1, 0)
                    V(lambda e: e.tensor_tensor(out=PB[64:128, 1, 15:GE], in0=PA[64:128, 1, 15:GE], in1=PA[64:128, 1, 7:GE - 8], op=ALU.add),
                      ["PA", ("PL", 0, 64)], ["PB"])
                    pool_out(PB, 1, 64)
                    V(lambda e: e.tensor_copy(out=PIN[:, :, 0:16], in_=PIN[:, :, G:G + 16]),
                      pin_keys + [("PL", 0, 0), ("PL", 0, 64), ("PL", 1, 0), ("PL", 1, 64)], ["PINh"])
                    for c in range(2):
                        cuk = [("CU", c), "CUh"]
                        V(lambda e, c=c: e.tensor_scalar(out=CY[:], in0=CU[:, c, 0:G], scalar1=small[:, 14 + 3 * c:15 + 3 * c],
                                                         scalar2=None, op0=ALU.mult), cuk + ["small"], ["CY"])
                        V(lambda e, c=c: e.scalar_tensor_tensor(out=CY[:], in0=CU[:, c, 1:1 + G], scalar=small[:, 15 + 3 * c:16 + 3 * c],
                                                                in1=CY[:], op0=ALU.mult, op1=ALU.add), cuk + ["small", "CY"], ["CY"])
                        V(lambda e, c=c: e.scalar_tensor_tensor(out=CY[:], in0=CU[:, c, 2:2 + G], scalar=small[:, 16 + 3 * c:17 + 3 * c],
                                                                in1=CY[:], op0=ALU.mult, op1=ALU.add), cuk + ["small", "CY"], ["CY"])
                        V(lambda e, c=c: e.tensor_tensor(out=CY[:], in0=CY[:], in1=CB[:, c, :], op=ALU.mult), ["CY", ("CB", c)], ["CY"])
                        V(lambda e, c=c: e.tensor_tensor(out=ogT[:, 4 + c, :], in0=CY[:], in1=GPC[:, 2 + c, :], op=ALU.mult),
                          ["CY", ("GPC", 2 + c)], [("ogT", 4 + c)])
                    V(lambda e: e.tensor_copy(out=CU[:, :, 0:2], in_=CU[:, :, G:G + 2]), [("CU", 0), ("CU", 1), "CUh", "CY"], ["CUh"])

                    for kt in range(nkt):
                        di = kt - tile0
                        for h in range(4):
                            pr, pb = h // 2, 64 * (h % 2)
                            bS = cnt["s"] % 3
                            cnt["s"] += 1
                            MM(psS[:, bS, 0:G], FK[pr][pb:pb + 64, kt * 128:(kt + 1) * 128], FQ[pr][pb:pb + 64, :],
                               True, True, [("FK%d" % pr, kt // QI), "FQ%d" % pr], [("psS", bS)])
                            flush(1)
                            pend.append([])
                            be = cnt["e"] % 3
                            cnt["e"] += 1
                            for qi in range(max(0, di), QI):
                                A(lambda e, bS=bS, be=be, qi=qi, h=h, kt=kt: e.activation(
                                    out=ET[be][:, qi * 128:(qi + 1) * 128], in_=psS[:, bS, qi * 128:(qi + 1) * 128],
                                    func=AF.Exp, scale=0.125, bias=BT[:, h, qi, kt:kt + 1]),
                                  [("psS", bS), "BT"], [("ET", be)])
                            if di >= 0:
                                V(lambda e, be=be, di=di: e.tensor_tensor(out=ET[be][:, di * 128:(di + 1) * 128],
                                                                          in0=ET[be][:, di * 128:(di + 1) * 128], in1=tri[:], op=ALU.mult),
                                  [("ET", be), "tri"], [("ET", be)])
                            for qi in range(max(0, di), QI):
                                defer_mm(slot(h * QI + qi, W), ET[be][:, qi * 128:(qi + 1) * 128], FV[:, kt, h, :],
                                         [("ET", be), ("FV", kt), "FV1"], akeys)
                    flush()
                    evac(akeys, OFOX, None, True)

                    S.enabled = STAGE >= 9
                    for (src, skey, goff, ch0) in ((ONSA, "ONSA", 0, 0), (OFOX, "OFOX", 256, 6)):
                        for c in range(2):
                            for qi in range(QI):
                                V(lambda e, src=src, qi=qi, goff=goff, c=c: e.tensor_tensor(
                                    out=OG[:, 0:128], in0=src[:, qi, c * 128:(c + 1) * 128],
                                    in1=GNF[:, qi, goff + c * 128:goff + (c + 1) * 128], op=ALU.mult),
                                  [(skey, qi), ("GNF", qi)], ["OG"])
                                MM(psM[:, qi * 128:(qi + 1) * 128], OG[:, 0:128], ident[:], True, True, ["OG", "ident"], ["psM"])
                            A(lambda e, ch=ch0 + c: e.activation(out=ogT[:, ch, :], in_=psM[:, 0:G], func=AF.Copy),
                              ["psM"], [("ogT", ch0 + c)])

                    S.enabled = STAGE >= 10
                    for c in range(2):
                        bank = cnt["p"] % 2
                        cnt["p"] += 1
                        MM(psP[:, bank, 0:G], pool_bd[:, c, :], PL[:, c, :], True, True,
                           ["pool_bd", ("PL", c, 0), ("PL", c, 64)], [("psP", bank)])
                        V(lambda e, c=c, bank=bank: e.scalar_tensor_tensor(out=ogT[:, 2 + c, :], in0=psP[:, bank, 0:G],
                                                                           scalar=small[:, 12 + c:13 + c], in1=GPC[:, c, :],
                                                                           op0=ALU.mult, op1=ALU.mult),
                          [("psP", bank), "small", ("GPC", c)], [("ogT", 2 + c)])
                    S.enabled = STAGE >= 11
                    S.enabled = STAGE >= 12
                    for c in range(8):
                        i, half = c // 2, c % 2
                        wt, wk = next_unit()
                        for qi in range(QI):
                            for kc in range(8):
                                MM(psP[:, 0, :], hT[:, kc, qi * 128:(qi + 1) * 128], wt[:, kc, :],
                                   kc == 0, kc == 7, [("hT", qi), wk], [("psP", 0)])
                            A(lambda e: e.activation(out=SM[:], in_=psP[:, 0, :], func=AF.Sigmoid), [("psP", 0)], ["SM"])
                            for fc in range(2):
                                MM(psP[:, 1, :], ogT[:, 2 * i + fc, qi * 128:(qi + 1) * 128], wt[:, 8 + fc, :],
                                   fc == 0, fc == 1, [("ogT", 2 * i + fc), wk], [("psP", 1)])
                            dsta = ACCM[:, qi, half * 512:(half + 1) * 512]
                            if i == 0:
                                V(lambda e, dsta=dsta: e.tensor_tensor(out=dsta, in0=psP[:, 1, :], in1=SM[:], op=ALU.mult),
                                  [("psP", 1), "SM"], [("ACCM", qi, half)])
                            else:
                                V(lambda e: e.tensor_tensor(out=TMPt[:], in0=psP[:, 1, :], in1=SM[:], op=ALU.mult),
                                  [("psP", 1), "SM"], ["TMPt"])
                                V(lambda e, dsta=dsta: e.tensor_tensor(out=dsta, in0=dsta, in1=TMPt[:], op=ALU.add),
                                  ["TMPt", ("ACCM", qi, half)], [("ACCM", qi, half)])
                    S.enabled = STAGE >= 13
                    for qi in range(QI):
                        V(lambda e, qi=qi: e.tensor_copy(out=ACCB[:], in_=ACCM[:, qi, :]), [("ACCM", qi, 0), ("ACCM", qi, 1)], ["ACCB"])
                        for c in range(8):
                            MM(psP[:, c // 4, (c % 4) * 128:(c % 4) * 128 + 128], ACCB[:, c * 128:(c + 1) * 128], ident[:],
                               True, True, ["ACCB", "ident"], [("psP", c // 4)])
                        for c in range(8):
                            A(lambda e, c=c, qi=qi: e.activation(
                                out=hT[:, c, qi * 128:(qi + 1) * 128],
                                in_=psP[:, c // 4, (c % 4) * 128:(c % 4) * 128 + 128], func=AF.Copy),
                              [("psP", c // 4)], [("hT", qi)])
                    for half in range(2):
                        wt, wk = next_unit()
                        for qi in range(QI):
                            bank = cnt["p"] % 2
                            cnt["p"] += 1
                            for fc in range(8):
                                MM(psP[:, bank, :], hT[:, fc, qi * 128:(qi + 1) * 128], wt[:, fc, :],
                                   fc == 0, fc == 7, [("hT", qi), wk], [("psP", bank)])
                            V(lambda e, bank=bank, qi=qi, half=half: e.tensor_tensor(
                                out=xg[:, qi, half * 512:(half + 1) * 512], in0=psP[:, bank, :],
                                in1=xg[:, qi, half * 512:(half + 1) * 512], op=ALU.add),
                              [("psP", bank), ("xg", qi)], [("xg", qi)])
                    S.enabled = True
                    if DEBUG and l == 0:
                        for qi in range(QI):
                            rr = r0 + t0 + qi * 128
                            S.dma(dbg[rr:rr + 128, 0:256], ONSA[:, qi, :], reads=[("ONSA", qi)], writes=[("dbg", g, qi)])
                            S.dma(dbg[rr:rr + 128, 256:320], IMP[:, qi, :], reads=[("IMP", qi)], writes=[("dbg2", g, qi)])
                            S.dma(dbg[rr:rr + 128, 320:328], m8a[:], reads=["m8a"], writes=[("dbg3", g, qi)])
                            S.dma(dbg[rr:rr + 128, 328:336], m8b[:], reads=["m8b"], writes=[("dbg4", g, qi)])
                            S.dma(dbg[rr:rr + 128, 336:400], IMPM2[:], reads=["IMPM2"], writes=[("dbg5", g, qi)])
                            S.dma(dbg[rr:rr + 128, 400:464], sel[:], reads=["sel"], writes=[("dbg6", g, qi)], eng="gpsimd")
                    for qi in range(QI):
                        rr = r0 + t0 + qi * 128
                        if not lastl:
                            S.dma(xdst[rr:rr + 128, :], xg[:, qi, :], reads=[("xg", qi)], writes=[("xd", l + 1, s, g, qi)])
                        else:
                            A(lambda e, qi=qi: e.activation(out=sq[:], in_=xg[:, qi, :], func=AF.Square),
                              [("xg", qi)], ["sq"])
                            V(lambda e, qi=qi: e.tensor_reduce(out=ss[:, qi:qi + 1], in_=sq[:], axis=mybir.AxisListType.X, op=ALU.add),
                              ["sq"], [("ss", qi)])
                            A(lambda e, qi=qi: e.activation(out=rstd[:, qi:qi + 1], in_=ss[:, qi:qi + 1], func=AF.Sqrt,
                                                            scale=1.0 / D, bias=epsb[:, 0:1]), [("ss", qi), "epsb"], [("rstd", qi)])
                            V(lambda e, qi=qi: e.reciprocal(out=rstd[:, qi:qi + 1], in_=rstd[:, qi:qi + 1]),
                              [("rstd", qi)], [("rstd", qi)])
                            V(lambda e, qi=qi: e.scalar_tensor_tensor(out=YO[:], in0=xg[:, qi, :], scalar=rstd[:, qi:qi + 1],
                                                                      in1=fgbc[:], op0=ALU.mult, op1=ALU.mult),
                              [("xg", qi), ("rstd", qi), "fgbc"], ["YO"])
                            S.dma(xdst[rr:rr + 128, :], YO[:], reads=["YO"], writes=[("yd", s, g, qi)])
        S.finalize()
    return nc


_CACHE = {}


def run_model(inp, T, DEPTH, n_cores, NSEQ, QI):
    consts = prep_consts(T, DEPTH, inp)
    layers = [prep_layer(inp, l) for l in range(DEPTH)]
    key = (T, DEPTH, NSEQ, QI)
    if key not in _CACHE:
        _CACHE[key] = build_program(T, DEPTH, NSEQ, QI, {k: v.shape for k, v in consts.items()},
                                    {k: layers[0][k].shape for k in LAYER_KEYS})
    nc = _CACHE[key]
    x = np.asarray(inp["x"], np.float32)
    B = x.shape[0]
    assert B == n_cores * NSEQ
    in_maps = []
    for c in range(n_cores):
        m = {"x": np.ascontiguousarray(x[c * NSEQ:(c + 1) * NSEQ].reshape(NSEQ * T, D))}
        for k, v in consts.items():
            m["c_" + k] = v
        for l in range(DEPTH):
            for k in LAYER_KEYS:
                m["l%d_%s" % (l, k)] = layers[l][k]
        in_maps.append(m)
    res = run_bass_kernel_spmd(nc, in_maps, core_ids=list(range(n_cores)), **RUN_KW)
    global LAST_RES
    LAST_RES = res
    out = np.stack([res.results[c]["y"].reshape(NSEQ, T, D) for c in range(n_cores)], 0)
    if DEBUG:
        global LAST_DBG
        LAST_DBG = [res.results[c]["dbg"] for c in range(n_cores)]
    return out.reshape(B, T, D).astype(np.float32)


def kernel(**inputs):
    return run_model(inputs, 4096, 2, N_CORES, 2, 2)
```
